# Optimizing a Trainium2 kernel written in Bass

```python
import math
import jax, jax.numpy as jnp
from jax import lax
import numpy as np

D_MODEL = 1024
BATCH = 4
SEQ = 4096
DEPTH = 4
DEC_BATCH = 32
DEC_SEQ = 16
PAST_LEN = 1024

CHUNK = 64
N_META = 16
D_MIX = D_MODEL
DQK = 64
DVA = 2 * DQK
W_A = D_MIX // 2
H_A = W_A // DVA
C_CONV = D_MIX // 4
CONV_W = 31
DK_C = 64
DV_C = 64
H_C = (D_MIX // 4) // DV_C
D_FF = 2816
FFN_CONV_W = 3
N_BUCKETS = 32
MAX_DIST = 128
Q_BLOCK = 128
REC_BLOCK = 16
EPS = 1e-6
NEG = -1e30
IN_SIZES = (H_A * 2 * DQK, H_A * 2 * DQK, W_A, 2 * C_CONV, H_C * DK_C, H_C * DV_C, H_C * DK_C, H_C * DV_C)
IN_COLS = sum(IN_SIZES)

kernel_name = "hymba_diffattn_conformer_hgrn2_stream"


def rmsnorm(x, g):
    xf = x.astype(jnp.float32)
    y = xf * lax.rsqrt(jnp.mean(xf * xf, axis=-1, keepdims=True) + EPS)
    return (y * g.astype(jnp.float32)).astype(x.dtype)


def layernorm(x, g, b):
    xf = x.astype(jnp.float32)
    mu = jnp.mean(xf, axis=-1, keepdims=True)
    xc = xf - mu
    var = jnp.mean(xc * xc, axis=-1, keepdims=True)
    y = xc * lax.rsqrt(var + EPS) * g.astype(jnp.float32) + b.astype(jnp.float32)
    return y.astype(x.dtype)


def rel_bucket(rel):
    nb = N_BUCKETS // 2
    max_exact = nb // 2
    ret = jnp.where(rel > 0, nb, 0)
    n = jnp.abs(rel)
    nf = jnp.maximum(n, 1).astype(jnp.float32)
    large = max_exact + (jnp.log(nf / max_exact) / math.log(MAX_DIST / max_exact) * (nb - max_exact)).astype(jnp.int32)
    large = jnp.minimum(large, nb - 1)
    return ret + jnp.where(n < max_exact, n, large)


def diff_attention(q, k, v, rel_bias, lam, q_pos, k_pos, q_cid, k_cid):
    scale = DQK ** -0.5
    table = rel_bias.astype(jnp.float32)

    def block(args):
        qb, qp, qc = args
        s = jnp.einsum('bqhcd,bkhcd->bhcqk', qb, k).astype(jnp.float32) * scale
        bias = table[rel_bucket(k_pos[None, :] - qp[:, None])]
        s = s + jnp.transpose(bias, (2, 0, 1))[None, :, None]
        mask = k_cid[None, :] <= qc[:, None]
        s = jnp.where(mask, s, NEG)
        p = jax.nn.softmax(s, axis=-1)
        a = p[:, :, 0] - lam * p[:, :, 1]
        return jnp.einsum('bhqk,bkhd->bqhd', a.astype(v.dtype), v)

    B, Tq = q.shape[0], q.shape[1]
    if Tq <= Q_BLOCK:
        return block((q, q_pos, q_cid))
    nb = -(-Tq // Q_BLOCK)
    pad = nb * Q_BLOCK - Tq
    qpad = jnp.pad(q, ((0, 0), (0, pad), (0, 0), (0, 0), (0, 0)))
    qpos = jnp.pad(q_pos, (0, pad), mode='edge').reshape(nb, Q_BLOCK)
    qcid = jnp.pad(q_cid, (0, pad), mode='edge').reshape(nb, Q_BLOCK)
    qb = qpad.reshape((B, nb, Q_BLOCK) + q.shape[2:]).swapaxes(0, 1)
    out = lax.map(block, (qb, qpos, qcid))
    out = out.swapaxes(0, 1).reshape((B, nb * Q_BLOCK) + out.shape[3:])
    return out[:, :Tq]


def causal_dwconv(x, past, w, b):
    xc = jnp.concatenate([past.astype(x.dtype), x], axis=1)
    y = lax.conv_general_dilated(xc, w[:, None, :].astype(x.dtype), window_strides=(1,), padding='VALID',
                                 dimension_numbers=('NWC', 'WIO', 'NWC'), feature_group_count=x.shape[-1])
    return y + b.astype(x.dtype), xc[:, xc.shape[1] - (w.shape[0] - 1):]


def hgrn2(q, log_f, k, v, S0):
    B, T, H, _ = q.shape
    nb = -(-T // REC_BLOCK)
    pad = nb * REC_BLOCK - T

    def prep(a):
        a = jnp.pad(a, ((0, 0), (0, pad), (0, 0), (0, 0)))
        return a.reshape(B, nb, REC_BLOCK, H, a.shape[-1]).transpose(1, 0, 3, 2, 4)

    qs, fs, ks, vs = prep(q), prep(log_f), prep(k), prep(v)
    causal = jnp.tril(jnp.ones((REC_BLOCK, REC_BLOCK), dtype=bool))

    def step(S, inp):
        qc, fc, kc, vc = inp
        bcum = jnp.cumsum(fc, axis=2)
        diff = bcum[:, :, :, None, :] - bcum[:, :, None, :, :]
        decay = jnp.where(causal[:, :, None], jnp.exp(jnp.minimum(diff, 0.0)), 0.0)
        attn = jnp.einsum('bhtd,bhsd,bhtsd->bhts', qc, kc, decay)
        o = jnp.einsum('bhts,bhsv->bhtv', attn, vc) + jnp.einsum('bhtd,bhdv->bhtv', qc * jnp.exp(bcum), S)
        bL = bcum[:, :, -1:, :]
        S_new = jnp.exp(bL[:, :, 0, :])[..., None] * S + jnp.einsum('bhsd,bhsv->bhdv', kc * jnp.exp(bL - bcum), vc)
        return S_new, o

    S, o = lax.scan(step, S0, (qs, fs, ks, vs))
    o = o.transpose(1, 0, 3, 2, 4).reshape(B, nb * REC_BLOCK, H, -1)[:, :T]
    return o, S


def trunk_layer(x, p, rel_bias, k_past, v_past, conv_past, S0, ffn_past, q_pos, k_pos, q_cid, k_cid):
    B, T = x.shape[0], x.shape[1]
    h = rmsnorm(x, p['g_mix'])
    proj = h @ p['w_in'].astype(x.dtype)
    offs = [int(o) for o in np.cumsum(IN_SIZES)[:-1]]
    qa, ka, va, glu, zf, ic, qc, gc = jnp.split(proj, offs, axis=-1)

    qa = rmsnorm(qa.reshape(B, T, H_A, 2, DQK), p['g_q'])
    ka = rmsnorm(ka.reshape(B, T, H_A, 2, DQK), p['g_k'])
    va = va.reshape(B, T, H_A, DVA)
    k_all = ka if k_past is None else jnp.concatenate([k_past.astype(x.dtype), ka], axis=1)
    v_all = va if v_past is None else jnp.concatenate([v_past.astype(x.dtype), va], axis=1)
    f32 = jnp.float32
    lam = (jnp.exp(jnp.sum(p['lq1'].astype(f32) * p['lk1'].astype(f32)))
           - jnp.exp(jnp.sum(p['lq2'].astype(f32) * p['lk2'].astype(f32))) + p['lam_init'])
    oa = diff_attention(qa, k_all, v_all, rel_bias, lam, q_pos, k_pos, q_cid, k_cid)
    oa = rmsnorm(oa, p['g_diff'].reshape(H_A, DVA)) * (1.0 - p['lam_init'])

    ga, gb = jnp.split(glu, 2, axis=-1)
    u = ga * jax.nn.sigmoid(gb)
    cu, conv_new = causal_dwconv(u, conv_past, p['conv_w'], p['conv_b'])
    ob = jax.nn.silu(layernorm(cu, p['ln_g'], p['ln_b']))

    z = zf.astype(f32).reshape(B, T, H_C, DK_C)
    lb = p['lb'].reshape(H_C, DK_C)
    log_f = jnp.logaddexp(jnp.log(lb), jnp.log1p(-lb) + jax.nn.log_sigmoid(z))
    kc = (1.0 - lb) * jax.nn.sigmoid(-z)
    oc, S_new = hgrn2(qc.astype(f32).reshape(B, T, H_C, DK_C), log_f, kc,
                      ic.astype(f32).reshape(B, T, H_C, DV_C), S0.astype(f32))
    oc = rmsnorm(oc, p['g_hgrn'].reshape(H_C, DV_C)) * jax.nn.silu(gc.astype(f32).reshape(B, T, H_C, DV_C))

    mix = jnp.concatenate([oa.reshape(B, T, W_A).astype(x.dtype), ob.astype(x.dtype),
                           oc.reshape(B, T, H_C * DV_C).astype(x.dtype)], axis=-1)
    x = x + mix @ p['w_out'].astype(x.dtype)

    h2 = rmsnorm(x, p['g_ffn'])
    gate, val = jnp.split(h2 @ p['w_up'].astype(x.dtype), 2, axis=-1)
    gate_c, ffn_new = causal_dwconv(gate, ffn_past, p['ffn_conv_w'], p['ffn_conv_b'])
    x = x + (jax.nn.gelu(gate_c) * val) @ p['w_down'].astype(x.dtype)
    return x, ka, va, conv_new, S_new, ffn_new


def setup_inputs(seed: int = 0) -> dict:
    key = jax.random.key(seed)
    ks = jax.random.split(key, 40)

    def nrm(k, shape, s):
        return jax.random.normal(k, shape, jnp.float32) * s

    return {
        'x_prompt': nrm(ks[0], (BATCH, SEQ, D_MODEL), 1.0),
        'x_sample': nrm(ks[1], (DEC_BATCH, DEC_SEQ, D_MODEL), 1.0),
        'cache_k': nrm(ks[2], (DEPTH, DEC_BATCH, PAST_LEN, H_A, 2, DQK), 1.0),
        'cache_v': nrm(ks[3], (DEPTH, DEC_BATCH, PAST_LEN, H_A, DVA), 1.0),
        'state_conv': nrm(ks[4], (DEPTH, DEC_BATCH, CONV_W - 1, C_CONV), 0.5),
        'state_hgrn': nrm(ks[5], (DEPTH, DEC_BATCH, H_C, DK_C, DV_C), 0.3),
        'state_ffn': nrm(ks[6], (DEPTH, DEC_BATCH, FFN_CONV_W - 1, D_FF), 0.7),
        'meta_tokens': nrm(ks[7], (N_META, D_MODEL), 1.0),
        'rel_bias': nrm(ks[8], (N_BUCKETS, H_A), 0.5),
        'g_mix': 1.0 + nrm(ks[9], (DEPTH, D_MODEL), 0.02),
        'w_in': nrm(ks[10], (DEPTH, D_MODEL, IN_COLS), D_MODEL ** -0.5),
        'g_q': 1.0 + nrm(ks[11], (DEPTH, DQK), 0.02),
        'g_k': 1.0 + nrm(ks[12], (DEPTH, DQK), 0.02),
        'lam_q1': nrm(ks[13], (DEPTH, DQK), 0.1),
        'lam_k1': nrm(ks[14], (DEPTH, DQK), 0.1),
        'lam_q2': nrm(ks[15], (DEPTH, DQK), 0.1),
        'lam_k2': nrm(ks[16], (DEPTH, DQK), 0.1),
        'g_diff': 1.0 + nrm(ks[17], (DEPTH, W_A), 0.02),
        'conv_w': nrm(ks[18], (DEPTH, CONV_W, C_CONV), CONV_W ** -0.5),
        'conv_b': nrm(ks[19], (DEPTH, C_CONV), 0.02),
        'ln_g': 1.0 + nrm(ks[20], (DEPTH, C_CONV), 0.02),
        'ln_b': nrm(ks[21], (DEPTH, C_CONV), 0.02),
        'lb_logits': nrm(ks[22], (DEPTH, H_C * DK_C), 0.5),
        'g_hgrn': 1.0 + nrm(ks[23], (DEPTH, H_C * DV_C), 0.02),
        'w_out': nrm(ks[24], (DEPTH, D_MIX, D_MODEL), D_MIX ** -0.5),
        'g_ffn': 1.0 + nrm(ks[25], (DEPTH, D_MODEL), 0.02),
        'w_up': nrm(ks[26], (DEPTH, D_MODEL, 2 * D_FF), D_MODEL ** -0.5),
        'ffn_conv_w': nrm(ks[27], (DEPTH, FFN_CONV_W, D_FF), FFN_CONV_W ** -0.5),
        'ffn_conv_b': nrm(ks[28], (DEPTH, D_FF), 0.02),
        'w_down': nrm(ks[29], (DEPTH, D_FF, D_MODEL), D_FF ** -0.5),
        'g_final': 1.0 + nrm(ks[30], (D_MODEL,), 0.02),
    }


def reference(x_prompt, x_sample, cache_k, cache_v, state_conv, state_hgrn, state_ffn,
              meta_tokens, rel_bias, g_mix, w_in, g_q, g_k, lam_q1, lam_k1, lam_q2, lam_k2, g_diff,
              conv_w, conv_b, ln_g, ln_b, lb_logits, g_hgrn, w_out, g_ffn, w_up, ffn_conv_w, ffn_conv_b,
              w_down, g_final):
    lb_all = jnp.cumsum(jax.nn.softmax(lb_logits.astype(jnp.float32), axis=0), axis=0)
    lb_all = lb_all - lb_all[0:1]

    Bp = x_prompt.shape[0]
    xp = jnp.concatenate([jnp.broadcast_to(meta_tokens.astype(x_prompt.dtype)[None], (Bp, N_META, D_MODEL)), x_prompt], axis=1)
    Tp = xp.shape[1]
    pos_p = jnp.arange(Tp, dtype=jnp.int32)
    cid_p = (pos_p + (CHUNK - N_META)) // CHUNK

    Bs, Ss = x_sample.shape[0], x_sample.shape[1]
    P = cache_k.shape[2]
    kpos_s = jnp.arange(P + Ss, dtype=jnp.int32)
    qpos_s = kpos_s[P:]
    kcid_s = kpos_s // CHUNK
    qcid_s = qpos_s // CHUNK

    xs = x_sample
    kp_l, vp_l, cp_l, hp_l, fp_l = [], [], [], [], []
    ks_l, vs_l, cs_l, hs_l, fs_l = [], [], [], [], []
    for l in range(DEPTH):
        p = dict(g_mix=g_mix[l], w_in=w_in[l], g_q=g_q[l], g_k=g_k[l], lq1=lam_q1[l], lk1=lam_k1[l],
                 lq2=lam_q2[l], lk2=lam_k2[l], lam_init=0.8 - 0.6 * math.exp(-0.3 * l), g_diff=g_diff[l],
                 conv_w=conv_w[l], conv_b=conv_b[l], ln_g=ln_g[l], ln_b=ln_b[l], lb=lb_all[l],
                 g_hgrn=g_hgrn[l], w_out=w_out[l], g_ffn=g_ffn[l], w_up=w_up[l],
                 ffn_conv_w=ffn_conv_w[l], ffn_conv_b=ffn_conv_b[l], w_down=w_down[l])
        xp, kn, vn, cn, hn, fn = trunk_layer(
            xp, p, rel_bias, None, None,
            jnp.zeros((Bp, CONV_W - 1, C_CONV), xp.dtype),
            jnp.zeros((Bp, H_C, DK_C, DV_C), jnp.float32),
            jnp.zeros((Bp, FFN_CONV_W - 1, D_FF), xp.dtype),
            pos_p, pos_p, cid_p, cid_p)
        kp_l.append(kn); vp_l.append(vn); cp_l.append(cn); hp_l.append(hn.astype(xp.dtype)); fp_l.append(fn)
        xs, kn, vn, cn, hn, fn = trunk_layer(
            xs, p, rel_bias, cache_k[l], cache_v[l], state_conv[l], state_hgrn[l], state_ffn[l],
            qpos_s, kpos_s, qcid_s, kcid_s)
        ks_l.append(kn); vs_l.append(vn); cs_l.append(cn.astype(state_conv.dtype))
        hs_l.append(hn.astype(state_hgrn.dtype)); fs_l.append(fn.astype(state_ffn.dtype))

    y_prompt = rmsnorm(xp, g_final)[:, N_META:]
    y_sample = rmsnorm(xs, g_final)
    return (y_prompt, y_sample,
            jnp.stack(kp_l), jnp.stack(vp_l), jnp.stack(cp_l), jnp.stack(hp_l), jnp.stack(fp_l),
            jnp.stack(ks_l), jnp.stack(vs_l), jnp.stack(cs_l), jnp.stack(hs_l), jnp.stack(fs_l))
```

```python
import math
import os
import numpy as np
import concourse.bass as bass
import concourse.mybir as mybir
from concourse.bass_utils import run_bass_kernel_spmd

F32 = mybir.dt.float32
BF16 = mybir.dt.bfloat16
AF = mybir.ActivationFunctionType
ALU = mybir.AluOpType
AX = mybir.AxisListType

EPOCH = 12000
DMA_EPOCH = 700
N_DMA_SEMS = 12


class Buf:
    __slots__ = ("name", "w", "r")

    def __init__(self, name=""):
        self.name = name
        self.w = None
        self.r = {}


class Prog:
    def __init__(self, nc):
        self.nc = nc
        self.eng_names = ("pe", "act", "dve", "pool", "sp")
        self.streams = {e: [] for e in self.eng_names}
        self.sem_handles = {}
        self.cur_sem = {}
        self.waited = {e: {} for e in self.eng_names}
        self.nsem = 0
        for e in self.eng_names:
            self._new_eng_sem(e)
        self.dma_sems = {q: [self._new_sem("d%s%d" % (q, i)) + [0] for i in range(N_DMA_SEMS)]
                         for q in ("sp", "pool", "act")}
        self.dma_rr = {q: 0 for q in ("sp", "pool", "act")}
        self.n_ops = {e: 0 for e in self.eng_names}
        self.n_waits = {e: 0 for e in self.eng_names}

    def _new_sem(self, name):
        self.nsem += 1
        key = "%s_%d" % (name, self.nsem)
        self.sem_handles[key] = self.nc.alloc_semaphore(key)
        return [key]

    def _new_eng_sem(self, e):
        key = self._new_sem("s" + e)[0]
        self.cur_sem[e] = [key, 0]

    def _collect(self, reads, writes):
        need = {}
        for b in reads:
            if b.w is not None:
                k, v = b.w
                if need.get(k, 0) < v:
                    need[k] = v
        for b in writes:
            if b.w is not None:
                k, v = b.w
                if need.get(k, 0) < v:
                    need[k] = v
            for k, v in b.r.items():
                if need.get(k, 0) < v:
                    need[k] = v
        return need

    def _waits_for(self, eng, need):
        ws = []
        wd = self.waited[eng]
        for k, v in need.items():
            if wd.get(k, 0) < v:
                wd[k] = v
                ws.append((k, v))
        return ws

    def _mark(self, tok, reads, writes):
        k, v = tok
        for b in reads:
            if b.r.get(k, 0) < v:
                b.r[k] = v
        for b in writes:
            b.w = tok
            b.r = {}

    enabled = True

    def section(self, k):
        import os
        if int(os.environ.get("PSTOP", "99")) <= k:
            self.enabled = False

    def bsection(self, k):
        import os
        if int(os.environ.get("BSTOP", "99")) <= k:
            self.enabled = False

    def op(self, eng, fn, reads=(), writes=()):
        if not self.enabled:
            return
        cs = self.cur_sem[eng]
        if cs[1] >= EPOCH:
            self._new_eng_sem(eng)
            cs = self.cur_sem[eng]
        need = self._collect(reads, writes)
        ws = self._waits_for(eng, need)
        if eng == "pe":
            ws = [w for w in ws if not w[0].startswith("spe_")]
        cs[1] += 1
        tok = (cs[0], cs[1])
        self.streams[eng].append((ws, fn, tok[0], 1))
        self._mark(tok, reads, writes)
        self.n_ops[eng] += 1
        self.n_waits[eng] += len(ws)
        return tok

    def dma(self, q, out_ap, in_ap, reads=(), writes=()):
        if not self.enabled:
            return
        lst = self.dma_sems[q]
        i = self.dma_rr[q]
        self.dma_rr[q] = (i + 1) % len(lst)
        ent = lst[i]
        if ent[1] >= DMA_EPOCH:
            ent = self._new_sem("d%s%d" % (q, i)) + [0]
            lst[i] = ent
        need = self._collect(reads, writes)
        if ent[1] > 0 and need.get(ent[0], 0) < ent[1] * 16:
            need[ent[0]] = ent[1] * 16
        ws = self._waits_for(q, need)
        ent[1] += 1
        tok = (ent[0], ent[1] * 16)

        def fn(e, out_ap=out_ap, in_ap=in_ap):
            return e.dma_start(out=out_ap, in_=in_ap)
        self.streams[q].append((ws, fn, tok[0], 16))
        self._mark(tok, reads, writes)
        self.n_ops[q] += 1
        self.n_waits[q] += len(ws)
        return tok

    def wait_all(self, eng, bufs):
        need = self._collect((), bufs)
        ws = self._waits_for(eng, need)
        if ws:
            self.streams[eng].append((ws, None, None, 0))

    def drain(self, eng="sp"):
        need = {}
        for e in self.eng_names:
            k, v = self.cur_sem[e]
            if v > 0:
                need[k] = v
        for q, lst in self.dma_sems.items():
            for ent in lst:
                if ent[1] > 0:
                    need[ent[0]] = ent[1] * 16
        ws = self._waits_for(eng, need)
        ws = [w for w in ws if w[0] != self.cur_sem[eng][0]]
        if ws:
            self.streams[eng].append((ws, None, None, 0))

    def build(self):
        nc = self.nc
        H = self.sem_handles

        def run(e, stream):
            for ws, fn, semkey, inc in stream:
                for k, v in ws:
                    e.wait_ge(H[k], v)
                if fn is not None:
                    fn(e).then_inc(H[semkey], inc)

        with nc.Block() as block:
            @block.tensor
            def _(e):
                run(e, self.streams["pe"])

            @block.scalar
            def _(e):
                run(e, self.streams["act"])

            @block.vector
            def _(e):
                run(e, self.streams["dve"])

            @block.gpsimd
            def _(e):
                run(e, self.streams["pool"])

            @block.sync
            def _(e):
                run(e, self.streams["sp"])


D = 1024
NMETA = 16
DFF = 2816
NFC = DFF // 128
CW = 31
INC = 3072
EPS = 1e-6
NEGM = -30000.0
SCALE = 0.125


class Cfg:
    def __init__(self, depth=4, seq=4096, past=1024, ns=4, nb=256):
        self.L = depth
        self.SEQ = seq
        self.PAST = past
        self.NS = ns
        self.NB = nb
        self.TP = NMETA + seq


def svec_layout(L):
    idx = {}
    n = 0

    def add(name, cnt):
        nonlocal n
        idx[name] = n
        n += cnt
    for l in range(L):
        add(("g_mix", l), 8)
        add(("g_ffn", l), 8)
        add(("g_diff", l), 4)
        add(("conv_b", l), 2)
        add(("ln_g", l), 2)
        add(("ln_b", l), 2)
        add(("g_hgrn", l), 2)
        add(("lbl", l), 2)
        add(("fcb", l), NFC)
        for t in range(3):
            add(("fcw", l, t), NFC)
        for t in range(CW):
            add(("cw", l, t), 2)
        add(("g_q", l), 1)
        add(("g_k", l), 1)
    add(("g_final",), 8)
    ngrp = (n + 127) // 128
    return idx, n, ngrp


def host_consts():
    c = {}
    c["ident"] = np.eye(128, dtype=np.float32)
    c["antiI"] = np.ascontiguousarray(np.eye(128, dtype=np.float32)[::-1])
    d = np.arange(512, dtype=np.int64) - 255
    nbk = 16
    me = 8
    ret = np.where(d > 0, nbk, 0)
    n = np.abs(d)
    nf = np.maximum(n, 1).astype(np.float32)
    large = me + (np.log(nf / np.float32(me)) / np.float32(math.log(128 / me)) * np.float32(nbk - me)).astype(np.int32)
    large = np.minimum(large, nbk - 1)
    bucket = ret + np.where(n < me, n, large)
    oh = np.zeros((32, 512), np.float32)
    oh[bucket[:511], np.arange(511)] = 1.0
    c["onehot"] = oh
    k = np.arange(128)[:, None]
    q = np.arange(128)[None, :]
    c["maskneg"] = np.where((k >= 64) & (q < 64), NEGM, 0.0).astype(np.float32)
    same = (k // 32) == (q // 32)
    c["tri"] = (same & (k <= q)).astype(np.float32)
    c["sa"] = (same & (k > q)).astype(np.float32)
    rm = np.ones((128, 1), np.float32)
    rm[64:96] = 0.0
    c["rowmask"] = rm
    return c


CONST_SHAPES = {"ident": [128, 128], "antiI": [128, 128], "onehot": [32, 512], "maskneg": [128, 128],
                "tri": [128, 128], "sa": [128, 128], "rowmask": [128, 1]}


def lam_init(l):
    return 0.8 - 0.6 * math.exp(-0.3 * l)


def build_program(cfg):
    L, SEQ, PAST, NS, NB, TP = cfg.L, cfg.SEQ, cfg.PAST, cfg.NS, cfg.NB, cfg.TP
    NPT = PAST // 128
    NT = max(1 + SEQ // 128, 1 + 2 * NPT + NS)
    KTW = max(TP, NMETA + 2 * PAST + 16 * NS)
    TT = TP + NS * 16
    nc = bass.Bass("TRN2", target_bir_lowering=False)
    P = Prog(nc)
    sidx, nrows, ngrp = svec_layout(L)

    def din(name, shape, dt=F32):
        return nc.dram_tensor(name, shape, dt, kind="ExternalInput").ap()

    def dout(name, shape):
        return nc.dram_tensor(name, shape, F32, kind="ExternalOutput").ap()

    def dscr(name, shape, dt):
        return nc.dram_tensor(name, shape, dt, kind="Internal").ap()

    xp_d = din("xp", [SEQ, D])
    meta_d = din("meta", [NMETA, D])
    xs_d = din("xs", [NS * 16, D])
    ck_d = din("ck", [L, NS, PAST, 512])
    cv_d = din("cv", [L, NS, PAST, 512])
    sc_d = din("sc", [L, NS, 30, 256])
    sh_d = din("sh", [L, NS, 256, 64])
    sf_d = din("sf", [L, NS, 2, DFF])
    w_in_d = din("w_in", [L, D, INC])
    w_out_d = din("w_out", [L, D, D])
    w_up_d = din("w_up", [L, D, 2 * DFF])
    w_dn_d = din("w_down", [L, DFF, D])
    svec_d = din("svec", [ngrp * 128, 128])
    relb_d = din("relb", [32, 4])
    lam_d = din("lamv", [4, L * 64])
    cst_d = {k: din("c_" + k, v) for k, v in CONST_SHAPES.items()}

    y_p = dout("y_p", [SEQ, D])
    y_s = dout("y_s", [NS * 16, D])
    k_p = dout("k_p", [L, TP, 512])
    v_p = dout("v_p", [L, TP, 512])
    conv_p = dout("conv_p", [L, 30, 256])
    hgrn_p = dout("hgrn_p", [L, 256, 64])
    ffn_p = dout("ffn_p", [L, 2, DFF])
    k_s = dout("k_s", [L, NS, 16, 512])
    v_s = dout("v_s", [L, NS, 16, 512])
    conv_s = dout("conv_s", [L, NS, 30, 256])
    hgrn_s = dout("hgrn_s", [L, NS, 256, 64])
    ffn_s = dout("ffn_s", [L, NS, 2, DFF])
    out_bufs = []
    DBG = os.environ.get("KDBG", "") != ""
    if DBG:
        dbg_mix = nc.dram_tensor("dbg_mix", [128, 8 * NB], BF16, kind="ExternalOutput").ap()
        dbg_x1 = dout("dbg_x1", [128, 8 * NB])
        dbg_h2 = nc.dram_tensor("dbg_h2", [128, 8 * NB], BF16, kind="ExternalOutput").ap()
        dbg_act = nc.dram_tensor("dbg_act", [128, NFC * NB], BF16, kind="ExternalOutput").ap()

    def obuf():
        b = Buf("out")
        out_bufs.append(b)
        return b

    NSL = 6 + 2 + NFC // 2 + 6
    wsl = dscr("wsl", [L, NSL, 128, 8 * 512], BF16)
    wbuf = [[Buf("wimg") for _ in range(NSL)] for l in range(L)]
    wb_in, wb_out, wb_up, wb_dn = w_in_d, w_out_d, w_up_d, w_dn_d
    xsc = dscr("xsc", [128, 8, TT], F32)
    xsc_b = {}
    tvd = dscr("tvd", [4, 512], F32)
    tvd_b = Buf("tvd")

    class TT_:
        pass

    def sb(name, shape, dt=F32):
        return nc.alloc_sbuf_tensor(name, shape, dt)

    class Rot:
        def __init__(self, name, n, shape, dt=F32):
            self.ts = [(sb("%s%d" % (name, i), shape, dt), Buf("%s%d" % (name, i))) for i in range(n)]
            self.i = 0

        def get(self):
            r = self.ts[self.i]
            self.i = (self.i + 1) % len(self.ts)
            return r

    ps = [nc.alloc_psum_tensor("ps%d" % i, [128, 512], F32) for i in range(8)]
    psb = [Buf("ps%d" % i) for i in range(8)]
    rot_ps_i = [0]

    def getps():
        i = 4 + rot_ps_i[0]
        rot_ps_i[0] = (rot_ps_i[0] + 1) % 4
        return ps[i], psb[i]

    def mm(out, lhsT, rhs, start, stop, R, W):
        P.op("pe", lambda e, o=out, l=lhsT, r=rhs, s=start, t=stop:
             e.matmul(o, lhsT=l, rhs=r, start=s, stop=t, skip_group_check=True), R, W)

    def tr(out, in_, ident, R, W):
        P.op("pe", lambda e, o=out, i=in_, d=ident: e.transpose(o, i, d), R, W)

    def act(out, in_, func, R, W, bias=None, scale=None):
        kw = {}
        if bias is not None:
            kw["bias"] = bias
        if scale is not None:
            kw["scale"] = scale
        P.op("act", lambda e, o=out, i=in_, f=func, kw=kw: e.activation(out=o, in_=i, func=f, **kw), R, W)

    def cp(eng, out, in_, R, W):
        if eng == "act":
            P.op("act", lambda e, o=out, i=in_: e.copy(out=o, in_=i), R, W)
        else:
            P.op(eng, lambda e, o=out, i=in_: e.tensor_copy(out=o, in_=i), R, W)

    def tt(eng, out, in0, in1, op, R, W):
        P.op(eng, lambda e, o=out, a=in0, b=in1, p=op: e.tensor_tensor(out=o, in0=a, in1=b, op=p), R, W)

    def ts(eng, out, in0, s1, s2, op0, op1, R, W):
        if s2 is None:
            P.op(eng, lambda e, o=out, a=in0, x=s1, p0=op0: e.tensor_scalar(out=o, in0=a, scalar1=x, scalar2=None, op0=p0), R, W)
        else:
            P.op(eng, lambda e, o=out, a=in0, x=s1, y=s2, p0=op0, p1=op1:
                 e.tensor_scalar(out=o, in0=a, scalar1=x, scalar2=y, op0=p0, op1=p1), R, W)

    def stt(eng, out, in0, scalar, in1, op0, op1, R, W):
        P.op(eng, lambda e, o=out, a=in0, s=scalar, b=in1, p0=op0, p1=op1:
             e.scalar_tensor_tensor(out=o, in0=a, scalar=s, in1=b, op0=p0, op1=p1), R, W)

    def mac(eng, acc, src, wcol, scratch, scratchb, R, W):
        if eng == "dve":
            stt("dve", acc, src, wcol, acc, ALU.mult, ALU.add, R + W, W)
        else:
            ts(eng, scratch, src, wcol, None, ALU.mult, None, R, [scratchb])
            tt(eng, acc, acc, scratch, ALU.add, [scratchb] + W, W)

    def memset(eng, ap, val, W):
        P.op(eng, lambda e, a=ap, v=val: e.memset(a, v), (), W)

    cst = {}
    cstb = Buf("consts")
    for k_, shp in CONST_SHAPES.items():
        cst[k_] = sb("sc_" + k_, shp)
        P.dma("sp", cst[k_][:], cst_d[k_][:, :], writes=[cstb])
    ident = cst["ident"]
    identb = sb("identb", [128, 128], BF16)
    cp("dve", identb[:], ident[:], [cstb], [cstb])
    ones_b = {}
    for nm, val in (("o1024", 1.0 / 1024), ("o128", 1.0 / 128), ("o256", 1.0 / 256), ("one", 1.0)):
        ones_b[nm] = sb("ones_" + nm, [128, 128], BF16)
        memset("pool", ones_b[nm][:], val, [cstb])
    blk64 = sb("blk64", [128, 128], BF16)
    memset("pool", blk64[:], 0.0, [cstb])
    memset("pool", blk64[0:64, 0:64], 1.0 / 64, [cstb])
    memset("pool", blk64[64:128, 64:128], 1.0 / 64, [cstb])

    P.section(1)
    vec = sb("vec", [128, ngrp * 128])
    vecb = Buf("vec")
    att = Rot("att", 2, [128, 128])
    svst = att
    xT = sb("xT", [128, 8, NB])
    xbc = [Buf("xT%d" % c) for c in range(8)]
    for g in range(ngrp):
        t_, b_ = svst.get()
        P.dma("sp", t_[:], svec_d[g * 128:(g + 1) * 128, :], writes=[b_])
        pt, pb = getps()
        tr(pt[:, 0:128], t_[:], ident[:], [b_, cstb], [pb])
        cp("dve", vec[:, g * 128:(g + 1) * 128], pt[:, 0:128], [pb], [vecb])

    def vcol(key, c=0):
        i = sidx[key] + c
        return vec[:, i:i + 1]

    for l in range(L):
        i = sidx[("g_diff", l)]
        ts("dve", vec[:, i:i + 4], vec[:, i:i + 4], 1.0 - lam_init(l), None, ALU.mult, None, [vecb], [vecb])

    P.section(2)
    lbT = sb("lbT", [128, 2, L])
    omlT = sb("omlT", [128, 2, L])
    lbtmp = sb("lbtmp", [128, 2, L])
    lbs = sb("lbs", [128, 2])
    for l in range(L):
        for c in range(2):
            act(lbtmp[:, c, l:l + 1], vcol(("lbl", l), c), AF.Exp, [vecb], [vecb])
    for c in range(2):
        P.op("dve", lambda e, c=c: e.reduce_sum(out=lbs[:, c:c + 1], in_=lbtmp[:, c, :], axis=AX.X), [vecb], [vecb])
    P.op("dve", lambda e: e.reciprocal(out=lbs[:], in_=lbs[:]), [vecb], [vecb])
    for c in range(2):
        ts("dve", lbtmp[:, c, :], lbtmp[:, c, :], lbs[:, c:c + 1], None, ALU.mult, None, [vecb], [vecb])
        memset("dve", lbT[:, c, 0:1], 0.0, [vecb])
        for l in range(1, L):
            tt("dve", lbT[:, c, l:l + 1], lbT[:, c, l - 1:l], lbtmp[:, c, l:l + 1], ALU.add, [vecb], [vecb])
    for c in range(2):
        ts("dve", omlT[:, c, :], lbT[:, c, :], -1.0, 1.0, ALU.mult, ALU.add, [vecb], [vecb])
    P.section(3)
    assert L * 64 <= NB
    lamt = xT[:, 0:4, 0:L * 64]
    neglam = sb("neglam", [128, L])
    lam2 = sb("lam2", [128, 2, L])
    for i in range(4):
        P.dma("sp", lamt[:, i, :], bass.AP(lam_d.tensor, i * L * 64, [[0, 128], [1, L * 64]]), writes=[vecb])
    tt("dve", lamt[:, 0, :], lamt[:, 0, :], lamt[:, 1, :], ALU.mult, [vecb], [vecb])
    tt("dve", lamt[:, 2, :], lamt[:, 2, :], lamt[:, 3, :], ALU.mult, [vecb], [vecb])
    for j, i in enumerate((0, 2)):
        for l in range(L):
            P.op("dve", lambda e, j=j, i=i, l=l: e.reduce_sum(out=lam2[:, j, l:l + 1], in_=lamt[:, i, l * 64:(l + 1) * 64], axis=AX.X),
                 [vecb], [vecb])
    act(lam2[:], lam2[:], AF.Exp, [vecb], [vecb])
    tt("dve", neglam[:], lam2[:, 1, :], lam2[:, 0, :], ALU.subtract, [vecb], [vecb])
    for l in range(L):
        ts("dve", neglam[:, l:l + 1], neglam[:, l:l + 1], -lam_init(l), None, ALU.add, None, [vecb], [vecb])

    P.section(4)
    relbc = sb("relbc", [128, 128])
    P.dma("sp", relbc[:], bass.AP(relb_d.tensor, 0, [[0, 128], [1, 128]]), writes=[vecb])
    relsb = sb("relsb", [32, 4])
    P.dma("sp", relsb[:], relb_d[:, :], writes=[vecb])
    pt, pb = getps()
    mm(pt[0:4, 0:512], relsb[:], cst["onehot"][:], True, True, [vecb, cstb], [pb])
    tvs = sb("tvs", [4, 512])
    cp("dve", tvs[:], pt[0:4, 0:512], [pb], [vecb])
    P.dma("sp", tvd[:, :], tvs[:], reads=[vecb], writes=[tvd_b])
    Tdiag = sb("Tdiag", [128, 4, 128])
    Tnear = sb("Tnear", [128, 4, 128])
    Tmeta = sb("Tmeta", [16, 4, 128])
    Tb = Buf("T")
    hank = att
    for h in range(4):
        for kind, base, nk in (("diag", 128, 128), ("near", 0, 128), ("meta", 112, 16)):
            ht, hb = hank.get()
            P.dma("sp", ht[:, 0:nk], bass.AP(tvd.tensor, h * 512 + base, [[1, 128], [1, nk]]), reads=[tvd_b], writes=[hb])
            pt, pb = getps()
            mm(pt[0:nk, 0:128], ht[:, 0:nk], cst["antiI"][:], True, True, [hb, cstb], [pb])
            if kind == "diag":
                tt("dve", Tdiag[:, h, :], pt[:, 0:128], cst["maskneg"][:], ALU.add, [pb, cstb], [Tb])
            elif kind == "near":
                cp("dve", Tnear[:, h, :], pt[:, 0:128], [pb], [Tb])
            else:
                cp("dve", Tmeta[:, h, :], pt[0:16, 0:128], [pb], [Tb])

    def cbias(h):
        return relbc[:, 15 * 4 + h:15 * 4 + h + 1]

    P.section(5)
    def convert_weights(l):
        for si, parts in enumerate(block_wplan(l)):
            for (l_, dr, k0, nk, c0, ncol, sc0) in parts:
                src = dr[l, k0 * 128:(k0 + nk) * 128, c0:c0 + ncol].rearrange("(kc p) c -> p kc c", p=128)
                dst = wsl[l, si].rearrange("p (kc c) -> p kc c", c=512)[:, 0:nk, sc0:sc0 + ncol]
                P.dma("pool", dst, src, reads=[], writes=[wbuf[l][si]])

    NSLOT = int(os.environ.get("NSLOT", "3"))
    WQ = os.environ.get("WQ", "sp").split(",")
    wslots = [(sb("wslot%d" % i, [128, 8, 512], BF16), Buf("wslot%d" % i)) for i in range(NSLOT)]
    wplan = []
    wstate = {"issued": 0, "used": 0}

    def block_wplan(l):
        pl = []
        for g in range(6):
            pl.append([(l, wb_in, 0, 8, g * 512, 512, 0)])
        for g in range(2):
            pl.append([(l, wb_out, 0, 8, g * 512, 512, 0)])
        for g in range(NFC // 2):
            pl.append([(l, wb_up, 0, 8, g * 256, 256, 0), (l, wb_up, 0, 8, DFF + g * 256, 256, 256)])
        for cg in range(2):
            for k0 in (0, 8, 16):
                pl.append([(l, wb_dn, k0, min(8, NFC - k0), cg * 512, 512, 0)])
        return pl

    def w_issue_upto(n):
        while wstate["issued"] < min(n, len(wplan)):
            i = wstate["issued"]
            st, sbuf_ = wslots[i % NSLOT]
            l, si, nkk = wplan[i]
            P.dma(WQ[i % len(WQ)], st[:].rearrange("p k c -> p (k c)")[:, 0:nkk * 512], wsl[l, si, :, 0:nkk * 512],
                  reads=[wbuf[l][si]], writes=[sbuf_])
            wstate["issued"] += 1

    def w_next():
        i = wstate["used"]
        w_issue_upto(i + NSLOT)
        wstate["used"] += 1
        return wslots[i % NSLOT]

    KTC = sb("KTC", [128, 4, KTW], BF16)
    VC = sb("VC", [128, NT, 512], BF16)
    ktb = [Buf("kt%d" % i) for i in range(NT)]
    hT = sb("hT", [128, 8, NB], BF16)
    hbc = [Buf("hT%d" % c) for c in range(8)]
    mixT = sb("mixT", [128, 8, NB], BF16)
    mixb = Buf("mixT")
    qT = sb("qT", [128, 4, 2 * NB], BF16)
    qb = [Buf("q%d" % h) for h in range(4)]
    for h_ in range(4):
        memset("pool", qT[:, h_, :], 0.0, [qb[h_]])
    sqr = Rot("sqr", 2, [128, NB], BF16)
    rsd = Rot("rsd", 2, [128, NB])
    tmp = Rot("tmp", 3, [128, NB])
    stg = Rot("stg", 2, [128, 512])
    Er = Rot("E", 3, [128, 2 * NB], BF16)
    ub = sb("ub", [128, 2, NB])
    ubb = Buf("ub")
    cacc = sb("cacc", [128, 2, NB])
    caccb = Buf("cacc")
    ubf = sb("ubf", [128, 2, NB], BF16)
    zfT = sb("zfT", [128, 2, NB])
    zfb = Buf("zfT")
    lfF = sb("lfF", [128, 2, NB])
    qcT = sb("qcT", [128, 2, NB])
    qcb = Buf("qcT")
    gsT = sb("gsT", [128, 2, NB])
    gsb = Buf("gsT")
    ebT = sb("ebT", [128, 2, NB])
    ebb = Buf("ebT")
    qtl = sb("qtl", [128, 2, NB])
    ktl = sb("ktl", [128, 2, NB])
    qkb = Buf("qk")
    NTM = max(1, NB // 128, 1 + NS)
    lfT = sb("lfT", [128, NTM, 256])
    omf = sb("omf", [128, NTM, 256])
    khz = sb("khz", [128, max(1, NB // 128), 256])
    vhT = sb("vhT", [128, NTM, 256])
    tmb = [Buf("tm%d" % i) for i in range(NTM)]
    NCH = NB // 32
    Sst = sb("Sst", [128, 2, NCH + 1, 64])
    Sb = Buf("S")
    ohT = sb("ohT", [128, 2, NB])
    ohb = Buf("ohT")
    bigbf = sb("bigbf", [128, max(NFC * NB, NPT * 512)], BF16)
    actT = bigbf[:, 0:NFC * NB].rearrange("p (c n) -> p c n", n=NB)
    actb = Buf("actT")
    gst = Rot("gst", 2, [128, 2 + NB])

    nseq = 1 + NS
    chalo = [sb("chalo%d" % i, [128, 2, 30]) for i in range(nseq)]
    fhalo = [sb("fhalo%d" % i, [128, NFC, 2]) for i in range(nseq)]
    Scur = [sb("Scur%d" % i, [128, 2, 64]) for i in range(nseq)]
    seqb = [Buf("seq%d" % i) for i in range(nseq)]

    kstage = bigbf[:, 0:NPT * 512].rearrange("p (t f) -> p t f", f=512)
    kstb = actb

    def norm_apply(src, N, ones_mat, gcol, outs, R, W, Rs=None, Ws=None):
        pt, pb = getps()
        n = len(src)
        for i, s in enumerate(src):
            sq, sqb = sqr.get()
            act(sq[:, 0:N], s, AF.Square, (Rs[i] if Rs else R), [sqb])
            mm(pt[:, 0:N], ones_mat[:], sq[:, 0:N], i == 0, i == n - 1, [sqb, cstb], [pb])
        rs, rsb = rsd.get()
        act(rs[:, 0:N], pt[:, 0:N], AF.Ln, [pb], [rsb], bias=EPS)
        act(rs[:, 0:N], rs[:, 0:N], AF.Exp, [rsb], [rsb], scale=-0.5)
        for i, s in enumerate(src):
            Ri = Rs[i] if Rs else R
            Wi = Ws[i] if Ws else W
            for o in outs[i]:
                if isinstance(o, tuple):
                    p0, p1, oap = o
                    stt("dve", oap, s[p0:p1], gcol[i][p0:p1], rs[p0:p1, 0:N], ALU.mult, ALU.mult, Ri + [rsb, vecb], Wi)
                else:
                    stt("dve", o, s, gcol[i], rs[:, 0:N], ALU.mult, ALU.mult, Ri + [rsb, vecb], Wi)

    def fm_chunk(wt, wtb, m, N, R):
        pt, pb = getps()
        for kc in range(8):
            mm(pt[:, 0:N], wt[:, kc, m * 128:(m + 1) * 128], hT[:, kc, 0:N], kc == 0, kc == 7, [wtb, hbc[kc]], [pb])
        return pt, pb

    def tm_proj(wt, wtb, c0, ntok, ncols, R):
        pt, pb = getps()
        for kc in range(8):
            mm(pt[0:ntok, 0:ncols], hT[:, kc, c0:c0 + ntok], wt[:, kc, 0:ncols], kc == 0, kc == 7, [wtb, hbc[kc]], [pb])
        return pt, pb

    def attention(l, N, q0, nq, ktiles, side=None, hold_last=False):
        flat = []
        for h in range(4):
            hs = []
            for kt in ktiles:
                vis = [b for b in kt["bias"] if b[2] != "skip"]
                if vis:
                    hs.append([h, kt, vis, False, False])
            hs[0][3] = True
            hs[-1][4] = True
            flat.extend(hs)
        LA = 2
        pend = {}

        def v3(ap2d, n):
            return ap2d.rearrange("p (c n) -> p c n", c=2) if n is None else ap2d

        def emit_st(i):
            h, kt, vis, _, _ = flat[i]
            v0 = vis[0][0]
            vn = nq - v0
            nk = kt["nk"]
            st_, stb = getps()
            qv = qT[:, h, :].rearrange("p (c n) -> p c n", c=2)[:, :, q0 + v0:q0 + v0 + vn]
            mm(st_[0:nk, 0:2 * vn], KTC[:, h, kt["kcol"]:kt["kcol"] + nk], qv,
               True, True, [kt["buf"], qb[h]], [stb])
            pend[i] = (st_, stb)

        defer = []

        def normalise(h, ob, zb):
            r0, r0b = tmp.get()
            act(r0[:, 0:nq], ps[zb][:, 0:nq], AF.Ln, [psb[zb]], [r0b])
            act(r0[:, 0:nq], r0[:, 0:nq], AF.Exp, [r0b], [r0b], scale=-1.0)
            tt("dve", r0[:, 0:nq], ps[ob][:, 0:nq], r0[:, 0:nq], ALU.mult, [psb[ob], r0b], [r0b])
            r1, r1b = tmp.get()
            act(r1[:, 0:nq], ps[zb][:, NB:NB + nq], AF.Ln, [psb[zb]], [r1b])
            act(r1[:, 0:nq], r1[:, 0:nq], AF.Exp, [r1b], [r1b], scale=-1.0)
            tt("dve", r1[:, 0:nq], ps[ob][:, NB:NB + nq], r1[:, 0:nq], ALU.mult, [psb[ob], r1b], [r1b])
            stt("dve", r0[:, 0:nq], r1[:, 0:nq], neglam[:, l:l + 1], r0[:, 0:nq], ALU.mult, ALU.add,
                [r0b, r1b, vecb], [r0b])
            norm_apply([r0[:, 0:nq]], nq, ones_b["o128"], [vcol(("g_diff", l), h)], [[mixT[:, h, q0:q0 + nq]]],
                       [r0b], [mixb])

        for i in range(min(LA, len(flat))):
            emit_st(i)
        for i, (h, kt, vis, first, last) in enumerate(flat):
            if i + LA < len(flat):
                emit_st(i + LA)
            if defer:
                defer[0][0] -= 1
                if defer[0][0] <= 0:
                    _, h_, ob_, zb_ = defer.pop(0)
                    normalise(h_, ob_, zb_)
            st_, stb = pend.pop(i)
            v0 = vis[0][0]
            vn = nq - v0
            nk = kt["nk"]
            ob, zb = 2 * (h % 2), 2 * (h % 2) + 1
            E, Eb = Er.get()
            sv = st_[0:nk, 0:2 * vn].rearrange("p (c n) -> p c n", c=2)
            Ev = E[0:nk, :].rearrange("p (c n) -> p c n", c=2)[:, :, 0:vn]
            for (c0, n, kind) in vis:
                o = c0 - v0
                if kind == "const":
                    act(Ev[:, :, o:o + n], sv[:, :, o:o + n], AF.Exp, [vecb], [Eb, stb],
                        bias=cbias(h)[0:nk, :], scale=SCALE)
                else:
                    Tm = {"diag": Tdiag, "near": Tnear, "meta": Tmeta}[kind]
                    t_, tb_ = tmp.get()
                    for c in range(2):
                        stt("dve", t_[0:nk, c * n:(c + 1) * n], st_[0:nk, c * vn + o:c * vn + o + n], SCALE, Tm[0:nk, h, 0:n],
                            ALU.mult, ALU.add, [Tb], [tb_, stb])
                        act(E[0:nk, c * NB + o:c * NB + o + n], t_[0:nk, c * n:(c + 1) * n], AF.Exp, [tb_], [Eb])
            if vn == NB:
                mm(ps[ob][:, :], VC[0:nk, kt["vt"], h * 128:(h + 1) * 128], E[0:nk, :], first, False, [kt["buf"], Eb], [psb[ob]])
                mm(ps[zb][:, :], ones_b["one"][0:nk, :], E[0:nk, :], first, False, [Eb, cstb], [psb[zb]])
            else:
                for c in range(2):
                    mm(ps[ob][:, c * NB + v0:c * NB + v0 + vn], VC[0:nk, kt["vt"], h * 128:(h + 1) * 128],
                       E[0:nk, c * NB:c * NB + vn], first and c == 0, False, [kt["buf"], Eb], [psb[ob]])
                    mm(ps[zb][:, c * NB + v0:c * NB + v0 + vn], ones_b["one"][0:nk, :],
                       E[0:nk, c * NB:c * NB + vn], first and c == 0, False, [Eb, cstb], [psb[zb]])
            if last:
                while defer:
                    _, h_, ob_, zb_ = defer.pop(0)
                    normalise(h_, ob_, zb_)
                defer.append([5, h, ob, zb])
            if side is not None:
                next(side, None)
                next(side, None)
        def flush():
            while defer:
                _, h_, ob_, zb_ = defer.pop(0)
                normalise(h_, ob_, zb_)
        if not hold_last:
            flush()
        return flush

    def conv_taps_gen(l, N, segs):
        for (seq, c0, n) in segs:
            for c in range(2):
                win, winb = cwin.get()
                cp("pool", win[:, 0:30], chalo[seq][:, c, :], [seqb[seq]], [winb])
                cp("pool", win[:, 30:30 + n], ub[:, c, c0:c0 + n], [ubb], [winb])
                cp("pool", chalo[seq][:, c, :], win[:, n:n + 30], [winb], [seqb[seq]])
                ts("dve", cacc[:, c, c0:c0 + n], win[:, 0:n], vcol(("cw", l, 0), c), vcol(("conv_b", l), c),
                   ALU.mult, ALU.add, [winb, vecb], [caccb])
                yield
                for j in range(1, CW):
                    mac("dve", cacc[:, c, c0:c0 + n], win[:, j:j + n], vcol(("cw", l, j), c), None, None,
                        [winb, vecb], [caccb])
                    yield

    def conv_ln(l, N):
        pt, pb = getps()
        for c in range(2):
            cp("act", ubf[:, c, 0:N], cacc[:, c, 0:N], [caccb], [ubb])
            mm(pt[:, 0:N], ones_b["o256"][:], ubf[:, c, 0:N], c == 0, c == 1, [ubb, cstb], [pb])
        for c in range(2):
            tt("dve", cacc[:, c, 0:N], cacc[:, c, 0:N], pt[:, 0:N], ALU.subtract, [caccb, pb], [caccb])
        norm_apply([cacc[:, 0, 0:N], cacc[:, 1, 0:N]], N, ones_b["o256"],
                   [vcol(("ln_g", l), 0), vcol(("ln_g", l), 1)],
                   [[cacc[:, 0, 0:N]], [cacc[:, 1, 0:N]]], [caccb], [caccb])
        for c in range(2):
            act(mixT[:, 4 + c, 0:N], cacc[:, c, 0:N], AF.Silu, [caccb, vecb], [mixb], bias=vcol(("ln_b", l), c))

    cwin = Rot("cwin", 2, [128, 30 + NB])

    def hgrn(l, N, segs, tmtiles):
        for (j, c0, n, seq) in tmtiles:
            for c in range(2):
                pt, pb = getps()
                mm(pt[:, 0:n], lfT[0:n, j, c * 128:(c + 1) * 128], cst["tri"][0:n, 0:n], True, True, [tmb[j], cstb], [pb])
                act(ebT[:, c, c0:c0 + n], pt[:, 0:n], AF.Exp, [pb], [ebb])
                t_, tb_ = tmp.get()
                act(t_[:, 0:n], pt[:, 0:n], AF.Exp, [pb], [tb_], scale=-1.0)
                tt("dve", ktl[:, c, c0:c0 + n], zfT[:, c, c0:c0 + n], t_[:, 0:n], ALU.mult, [zfb, tb_], [qkb])
                tt("dve", qtl[:, c, c0:c0 + n], qcT[:, c, c0:c0 + n], ebT[:, c, c0:c0 + n], ALU.mult, [qcb, ebb], [qkb])
            pt, pb = getps()
            mm(pt[0:n, 0:256], cst["sa"][0:n, 0:n], lfT[0:n, j, :], True, True, [tmb[j], cstb], [pb])
            t_, tb_ = tmp.get()
            act(t_[0:n, 0:256], pt[0:n, 0:256], AF.Exp, [pb], [tb_])
            tt("dve", omf[0:n, j, :], omf[0:n, j, :], t_[0:n, 0:256], ALU.mult, [tmb[j], tb_], [tmb[j]])
            if n == 128:
                ts("dve", khz[64:128, j, :], omf[64:128, j, :], cst["rowmask"][64:128, 0:1], None, ALU.mult, None,
                   [tmb[j], cstb], [tmb[j]])
        def tmap(t):
            for (j_, c0_, n_, s_) in tmtiles:
                if c0_ <= t < c0_ + n_:
                    return j_, t - c0_
            raise AssertionError
        for (seq, c0, n) in segs:
            nch = (n + 31) // 32
            for p in range(2):
                cp("dve", Sst[:, p, 0, :], Scur[seq][:, p, :], [seqb[seq]], [Sb])
            for ch in range(nch):
                t0 = c0 + ch * 32
                nt = min(32, n - ch * 32)
                j, r0 = tmap(t0)
                pt, pb = getps()
                for hh in range(4):
                    p, lo = hh // 2, 64 * (hh % 2)
                    if r0 == 96:
                        lhs = khz[64:128, j, hh * 64:(hh + 1) * 64]
                        rhs = vhT[64:128, j, hh * 64:(hh + 1) * 64]
                    else:
                        lhs = omf[r0:r0 + nt, j, hh * 64:(hh + 1) * 64]
                        rhs = vhT[r0:r0 + nt, j, hh * 64:(hh + 1) * 64]
                    mm(pt[lo:lo + 64, p * 64:(p + 1) * 64], lhs, rhs, True, True, [tmb[j]], [pb])
                for p in range(2):
                    stt("dve", Sst[:, p, ch + 1, :], Sst[:, p, ch, :], ebT[:, p, t0 + nt - 1:t0 + nt],
                        pt[:, p * 64:(p + 1) * 64], ALU.mult, ALU.add, [Sb, ebb, pb], [Sb])
            for p in range(2):
                cp("dve", Scur[seq][:, p, :], Sst[:, p, nch, :], [Sb], [seqb[seq]])
            for t0 in range(c0, c0 + n, 128):
                nt = min(128, c0 + n - t0)
                j, r0 = tmap(t0)
                ot, otb = ps[0], psb[0]
                for hh in range(4):
                    p, lo = hh // 2, 64 * (hh % 2)
                    pt, pb = getps()
                    mm(pt[r0:r0 + nt, 0:nt], ktl[lo:lo + 64, p, t0:t0 + nt], qtl[lo:lo + 64, p, t0:t0 + nt], True, True,
                       [qkb], [pb])
                    a_, ab_ = att.get()
                    tt("dve", a_[r0:r0 + nt, 0:nt], pt[r0:r0 + nt, 0:nt], cst["tri"][r0:r0 + nt, r0:r0 + nt], ALU.mult,
                       [pb, cstb], [ab_])
                    mm(ot[lo:lo + 64, p * 128:p * 128 + nt], vhT[r0:r0 + nt, j, hh * 64:(hh + 1) * 64], a_[r0:r0 + nt, 0:nt],
                       True, False, [tmb[j], ab_], [otb])
                    for ch in range((nt + 31) // 32):
                        cc = (t0 - c0) // 32 + ch
                        n32 = min(32, nt - ch * 32)
                        mm(ot[lo:lo + 64, p * 128 + ch * 32:p * 128 + ch * 32 + n32], Sst[lo:lo + 64, p, cc, :],
                           qtl[lo:lo + 64, p, t0 + ch * 32:t0 + ch * 32 + n32], False, False, [Sb, qkb], [otb])
                for p in range(2):
                    cp("act", ohT[:, p, t0:t0 + nt], ot[:, p * 128:p * 128 + nt], [otb], [ohb])
        for p in range(2):
            norm_apply([ohT[:, p, 0:N]], N, blk64, [vcol(("g_hgrn", l), p)], [[ohT[:, p, 0:N]]], [ohb], [ohb])
            tt("dve", mixT[:, 6 + p, 0:N], ohT[:, p, 0:N], gsT[:, p, 0:N], ALU.mult, [ohb, gsb], [mixb])

    def tm_out(srcs, ntok, dst_ap, R):
        for g0 in range(0, len(srcs), 4):
            grp = srcs[g0:g0 + 4]
            pt, pb = getps()
            for i, s in enumerate(grp):
                tr(pt[0:ntok, i * 128:(i + 1) * 128], s, ident[:], R + [cstb], [pb])
            st_, stb = stg.get()
            cp("act", st_[0:ntok, 0:128 * len(grp)], pt[0:ntok, 0:128 * len(grp)], [pb], [stb])
            P.dma("act", dst_ap[:, g0 * 128:g0 * 128 + 128 * len(grp)], st_[0:ntok, 0:128 * len(grp)], reads=[stb], writes=[obuf()])

    def fm_in(src_ap, ntok, nchunks, dst_fn, W, q="sp", Wfn=None):
        for g0 in range(0, nchunks, 4):
            ng = min(4, nchunks - g0)
            xi, xib = stg.get()
            P.dma(q, xi[0:ntok, 0:128 * ng], src_ap[:, g0 * 128:(g0 + ng) * 128], writes=[xib])
            pt, pb = getps()
            for i in range(ng):
                tr(pt[:, i * 128:i * 128 + ntok], xi[0:ntok, i * 128:(i + 1) * 128], ident[0:ntok, 0:ntok],
                   [xib, cstb], [pb])
            for i in range(ng):
                cp("dve", dst_fn(g0 + i), pt[:, i * 128:i * 128 + ntok], [pb], (Wfn(g0 + i) if Wfn else W))

    def run_block(l, blk):
        N = blk["N"]
        xc0 = blk["xcol"]
        segs = blk["segs"]
        tmt = blk["tmt"]
        key = (blk["id"],)
        if key not in xsc_b:
            xsc_b[key] = [Buf("xscA"), Buf("xscB")]
        if l == 0:
            for (j, c0, n, seq, vt, kcol, src) in blk["x0"]:
                fm_in(src, n, 8, lambda c, c0=c0, n=n: xT[:, c, c0:c0 + n], None, Wfn=lambda c: [xbc[c]])
        else:
            for hf in range(2):
                P.dma("sp", xT[:, 4 * hf:4 * hf + 4, 0:N], xsc[:, 4 * hf:4 * hf + 4, xc0:xc0 + N],
                      reads=[xsc_b[key][hf]], writes=xbc[4 * hf:4 * hf + 4])
        P.bsection(1)
        norm_apply([xT[:, c, 0:N] for c in range(8)], N, ones_b["o1024"], [vcol(("g_mix", l), c) for c in range(8)],
                   [[hT[:, c, 0:N]] for c in range(8)], None, None,
                   Rs=[[xbc[c]] for c in range(8)], Ws=[[hbc[c]] for c in range(8)])
        P.bsection(2)
        wt, wtb = w_next()
        for h in range(4):
            pt, pb = fm_chunk(wt, wtb, h, N, [])
            norm_apply([pt[:, 0:N]], N, blk64, [vcol(("g_q", l))],
                       [[(0, 64, qT[0:64, h, 0:N]), (64, 128, qT[64:128, h, NB:NB + N])]], [pb], [qb[h]])
        P.bsection(3)
        wt, wtb = w_next()
        kn_l = []
        for h in range(4):
            pt, pb = fm_chunk(wt, wtb, h, N, [])
            kn, knb = knr.get()
            outs = [kn[:, 0:N]]
            norm_apply([pt[:, 0:N]], N, blk64, [vcol(("g_k", l))], [outs], [pb], [knb])
            for (j, c0, n, seq, vt, kcol, *_) in tmt:
                cp("pool", KTC[:, h, kcol:kcol + n], kn[:, c0:c0 + n], [knb], [ktb[vt]])
            kn_l.append((kn, knb))
        P.bsection(4)
        wt, wtb = w_next()
        for (j, c0, n, seq, vt, kcol, *_) in tmt:
            VSK = os.environ.get("VSKIP", "")
            pt, pb = tm_proj(wt, wtb, c0, n, 512, []) if "m" not in VSK else getps()
            st_, stb = stg.get()
            if "a" not in VSK:
                cp("act", st_[0:n, :], pt[0:n, :], [pb], [stb])
            if "d" not in VSK:
                cp("dve", VC[0:n, vt, :], st_[0:n, :], [stb], [ktb[vt]])
            if "o" not in VSK:
                P.dma("act", blk["vdst"](l, seq, c0, n), st_[0:n, :], reads=[stb], writes=[obuf()])
        P.bsection(5)
        wt, wtb = w_next()
        pga = [fm_chunk(wt, wtb, m, N, []) for m in range(2)]
        for c in range(2):
            pt, pb = fm_chunk(wt, wtb, 2 + c, N, [])
            t_, tb_ = tmp.get()
            act(t_[:, 0:N], pt[:, 0:N], AF.Sigmoid, [pb], [tb_])
            tt("dve", ub[:, c, 0:N], pga[c][0][:, 0:N], t_[:, 0:N], ALU.mult, [pga[c][1], tb_], [ubb])
        P.bsection(6)
        wt, wtb = w_next()
        for c in range(2):
            pt, pb = fm_chunk(wt, wtb, c, N, [])
            act(zfT[:, c, 0:N], pt[:, 0:N], AF.Sigmoid, [pb], [zfb])
            ts("dve", zfT[:, c, 0:N], zfT[:, c, 0:N], omlT[:, c, l:l + 1], lbT[:, c, l:l + 1], ALU.mult, ALU.add,
               [zfb, vecb], [zfb])
            act(lfF[:, c, 0:N], zfT[:, c, 0:N], AF.Ln, [zfb], [zfb])
            ts("dve", zfT[:, c, 0:N], zfT[:, c, 0:N], -1.0, 1.0, ALU.mult, ALU.add, [zfb], [zfb])
        for (j, c0, n, seq, vt, kcol, *_) in tmt:
            pt, pb = getps()
            for c in range(2):
                tr(pt[0:n, c * 128:(c + 1) * 128], lfF[:, c, c0:c0 + n], ident[:], [zfb, cstb], [pb])
                tr(pt[0:n, 256 + c * 128:256 + (c + 1) * 128], zfT[:, c, c0:c0 + n], ident[:], [zfb, cstb], [pb])
            cp("act", lfT[0:n, j, :], pt[0:n, 0:256], [pb], [tmb[j]])
            cp("act", omf[0:n, j, :], pt[0:n, 256:512], [pb], [tmb[j]])
            pt, pb = getps()
            for kc in range(8):
                mm(pt[0:n, 0:256], hT[:, kc, c0:c0 + n], wt[:, kc, 256:512], kc == 0, kc == 7, [wtb, hbc[kc]], [pb])
            cp("act", vhT[0:n, j, :], pt[0:n, 0:256], [pb], [tmb[j]])
        P.bsection(7)
        wt, wtb = w_next()
        for c in range(2):
            pt, pb = fm_chunk(wt, wtb, c, N, [])
            cp("act", qcT[:, c, 0:N], pt[:, 0:N], [pb], [qcb])
        for c in range(2):
            pt, pb = fm_chunk(wt, wtb, 2 + c, N, [])
            act(gsT[:, c, 0:N], pt[:, 0:N], AF.Silu, [pb], [gsb])
        for (j, c0, n, seq, vt, kcol, *_) in tmt:
            dst = blk["kdst"](l, seq, c0, n)
            tm_out([kn[:, c0:c0 + n] for (kn, knb) in kn_l], n, dst, [knb for (kn, knb) in kn_l])
        P.bsection(8)
        side = conv_taps_gen(l, N, segs)
        late_flush = None
        for ai, a in enumerate(blk["attn"]):
            fl = attention(l, N, a["q0"], a["nq"], a["ktiles"](l), side, hold_last=(ai == len(blk["attn"]) - 1))
            if ai == len(blk["attn"]) - 1:
                late_flush = fl
            if a.get("post"):
                a["post"](l)
        for _ in side:
            pass
        P.bsection(9)
        conv_ln(l, N)
        P.bsection(10)
        hgrn(l, N, segs, [(j, c0, n, seq) for (j, c0, n, seq, *_) in tmt])
        if late_flush is not None:
            late_flush()
        P.bsection(11)
        for g in range(2):
            wt, wtb = w_next()
            for m in range(4):
                pt, pb = getps()
                for kc in range(8):
                    mm(pt[:, 0:N], wt[:, kc, m * 128:(m + 1) * 128], mixT[:, kc, 0:N], kc == 0, kc == 7, [wtb, mixb], [pb])
                tt("dve", xT[:, g * 4 + m, 0:N], xT[:, g * 4 + m, 0:N], pt[:, 0:N], ALU.add, [xbc[g * 4 + m], pb], [xbc[g * 4 + m]])
        if DBG and blk["id"] == "pm" and l == 0:
            P.dma("sp", dbg_mix.rearrange("p (c n) -> p c n", n=NB), mixT[:], reads=[mixb], writes=[obuf()])
        P.bsection(12)
        norm_apply([xT[:, c, 0:N] for c in range(8)], N, ones_b["o1024"], [vcol(("g_ffn", l), c) for c in range(8)],
                   [[hT[:, c, 0:N]] for c in range(8)], None, None,
                   Rs=[[xbc[c]] for c in range(8)], Ws=[[hbc[c]] for c in range(8)])
        P.bsection(13)
        for g in range(NFC // 2):
            wg, wgb = w_next()
            for m in range(2):
                fc = g * 2 + m
                pg, pgb = fm_chunk(wg, wgb, m, N, [])
                pv, pvb = fm_chunk(wg, wgb, 2 + m, N, [])
                gs_, gsb_ = gst.get()
                t_, tb_ = tmp.get()
                for si, (seq, c0, n) in enumerate(segs):
                    o = si * (2 + n)
                    cp("pool", gs_[:, o:o + 2], fhalo[seq][:, fc, :], [seqb[seq]], [gsb_])
                    cp("act", gs_[:, o + 2:o + 2 + n], pg[:, c0:c0 + n], [pgb], [gsb_])
                    ts("dve", t_[:, c0:c0 + n], gs_[:, o + 2:o + 2 + n], vcol(("fcw", l, 2), fc), vcol(("fcb", l), fc),
                       ALU.mult, ALU.add, [gsb_, vecb], [tb_])
                    for tap in (1, 0):
                        mac("dve", t_[:, c0:c0 + n], gs_[:, o + tap:o + tap + n], vcol(("fcw", l, tap), fc), None, None,
                            [gsb_, vecb], [tb_])
                    cp("act", fhalo[seq][:, fc, :], pg[:, c0 + n - 2:c0 + n], [pgb, gsb_], [seqb[seq]])
                act(t_[:, 0:N], t_[:, 0:N], AF.Gelu_apprx_tanh, [tb_], [tb_])
                tt("dve", actT[:, fc, 0:N], t_[:, 0:N], pv[:, 0:N], ALU.mult, [tb_, pvb], [actb])
        if DBG and blk["id"] == "pm" and l == 0:
            P.dma("sp", dbg_h2.rearrange("p (c n) -> p c n", n=NB), hT[:], reads=list(hbc), writes=[obuf()])
            P.dma("sp", dbg_act.rearrange("p (c n) -> p c n", n=NB), actT, reads=[actb], writes=[obuf()])
        P.bsection(14)
        for cg in range(2):
            for ki, k0 in enumerate((0, 8, 16)):
                wt, wtb = w_next()
                nk = min(8, NFC - k0)
                for m in range(4):
                    for kc in range(nk):
                        mm(ps[m][:, 0:N], wt[:, kc, m * 128:(m + 1) * 128], actT[:, k0 + kc, 0:N],
                           ki == 0 and kc == 0, False, [wtb, actb], [psb[m]])
            for m in range(4):
                tt("dve", xT[:, cg * 4 + m, 0:N], xT[:, cg * 4 + m, 0:N], ps[m][:, 0:N], ALU.add, [xbc[cg * 4 + m], psb[m]], [xbc[cg * 4 + m]])
        if DBG and blk["id"] == "pm" and l == 0:
            P.dma("sp", dbg_x1.rearrange("p (c n) -> p c n", n=NB), xT[:], reads=list(xbc), writes=[obuf()])
        P.bsection(15)
        if l < L - 1:
            for hf in range(2):
                P.dma("sp", xsc[:, 4 * hf:4 * hf + 4, xc0:xc0 + N], xT[:, 4 * hf:4 * hf + 4, 0:N],
                      reads=xbc[4 * hf:4 * hf + 4], writes=[xsc_b[key][hf]])
        else:
            if blk["ydst"] is not None:
                gf = sidx[("g_final",)]
                norm_apply([xT[:, c, 0:N] for c in range(8)], N, ones_b["o1024"], [vec[:, gf + c:gf + c + 1] for c in range(8)],
                           [[xT[:, c, 0:N]] for c in range(8)], None, None,
                           Rs=[[xbc[c]] for c in range(8)], Ws=[[xbc[c]] for c in range(8)])
                for (j, c0, n, seq, *_) in tmt:
                    ydst_ = blk["ydst"](seq, c0, n)
                    if ydst_ is not None:
                        tm_out([xT[:, c, c0:c0 + n] for c in range(8)], n, ydst_, list(xbc))

    knr = Rot("kn", 4, [128, NB])

    def end_of_layer_states(l):
        for seq in range(nseq):
            if seq == 0:
                cdst, hdst, fdst = conv_p[l], hgrn_p[l], ffn_p[l]
            else:
                cdst, hdst, fdst = conv_s[l, seq - 1], hgrn_s[l, seq - 1], ffn_s[l, seq - 1]
            tm_out([chalo[seq][:, c, :] for c in range(2)], 30, cdst, [seqb[seq]])
            for p in range(2):
                P.dma("sp", hdst[p * 128:(p + 1) * 128, :], Scur[seq][:, p, :], reads=[seqb[seq]], writes=[obuf()])
            tm_out([fhalo[seq][:, fc, :] for fc in range(NFC)], 2, fdst, [seqb[seq]])

    def load_states(l):
        for seq in range(nseq):
            if seq == 0:
                memset("pool", chalo[0][:], 0.0, [seqb[0]])
                memset("pool", fhalo[0][:], 0.0, [seqb[0]])
                memset("pool", Scur[0][:], 0.0, [seqb[0]])
            else:
                s = seq - 1
                fm_in(sc_d[l, s], 30, 2, lambda c, seq=seq: chalo[seq][:, c, :], [seqb[seq]])
                for p in range(2):
                    P.dma("sp", Scur[seq][:, p, :], sh_d[l, s, p * 128:(p + 1) * 128, :], writes=[seqb[seq]])
                for g0 in range(0, NFC, 8):
                    ng = min(8, NFC - g0)
                    fm_in(sf_d[l, s, :, g0 * 128:(g0 + ng) * 128], 2, ng,
                          lambda c, seq=seq, g0=g0: fhalo[seq][:, g0 + c, :], [seqb[seq]])

    blocks = []
    NQT = NB // 128

    def prompt_ktiles(qtiles):
        def f(l):
            kts = []
            if qtiles[0][2] < 0:
                kts.append(dict(kcol=0, nk=16, vt=0, buf=ktb[0], bias=[(0, 16, "diag")]))
                return kts
            kts.append(dict(kcol=0, nk=16, vt=0, buf=ktb[0],
                            bias=[(c0, n, "meta" if gi == 0 else "const") for (c0, n, gi) in qtiles]))
            last = qtiles[-1][2]
            for kt in range(last + 1):
                bias = []
                for (c0, n, gi) in qtiles:
                    if kt > gi:
                        kind = "skip"
                    elif kt == gi:
                        kind = "diag"
                    elif kt == gi - 1:
                        kind = "near"
                    else:
                        kind = "const"
                    bias.append((c0, n, kind))
                kts.append(dict(kcol=NMETA + kt * 128, nk=128, vt=1 + kt, buf=ktb[1 + kt], bias=bias))
            return kts
        return f

    XOFF = NMETA + 16 * NS

    def sample_ktiles(s):
        reg = s % 2
        kbase = NMETA + reg * PAST
        vbase = 1 + reg * NPT

        def f(l):
            kts = []
            for kt in range(NPT):
                kind = "near" if kt == NPT - 1 else "const"
                kts.append(dict(kcol=kbase + kt * 128, nk=128, vt=vbase + kt, buf=ktb[vbase + kt], bias=[(0, 16, kind)]))
            kts.append(dict(kcol=NMETA + 2 * PAST + 16 * s, nk=16, vt=1 + 2 * NPT + s, buf=ktb[1 + 2 * NPT + s],
                            bias=[(0, 16, "diag")]))
            return kts
        return f

    f_tmt = [(0, 0, NMETA, 0, 0, 0)]
    f_x0 = [(0, 0, NMETA, 0, 0, 0, meta_d[:, :])]
    for s in range(NS):
        c0s = NMETA + 16 * s
        f_tmt.append((1 + s, c0s, 16, 1 + s, 1 + 2 * NPT + s, NMETA + 2 * PAST + 16 * s))
        f_x0.append((1 + s, c0s, 16, 1 + s, 0, 0, xs_d[s * 16:(s + 1) * 16, :]))

    def mk_sattn(s):
        def post(l):
            if s + 2 < NS:
                load_sample_cache(l, s + 2)
        return dict(q0=NMETA + 16 * s, nq=16, ktiles=sample_ktiles(s), post=post)

    blocks.append(dict(id="pm", N=XOFF, xcol=0, segs=[(0, 0, NMETA)] + [(1 + s, NMETA + 16 * s, 16) for s in range(NS)],
                       tmt=f_tmt, x0=f_x0,
                       attn=[dict(q0=0, nq=NMETA, ktiles=prompt_ktiles([(0, NMETA, -1)]))] + [mk_sattn(s) for s in range(NS)],
                       kdst=lambda l, seq, c0, n: (k_p[l, 0:NMETA, :] if seq == 0 else k_s[l, seq - 1]),
                       vdst=lambda l, seq, c0, n: (v_p[l, 0:NMETA, :] if seq == 0 else v_s[l, seq - 1]),
                       ydst=lambda seq, c0, n: (None if seq == 0 else y_s[(seq - 1) * 16:seq * 16, :])))
    for b in range(SEQ // NB):
        t0 = NMETA + b * NB
        tmt = [(j, j * 128, 128, 0, 1 + b * NQT + j, t0 + j * 128) for j in range(NQT)]
        x0 = [(j, j * 128, 128, 0, 0, 0, xp_d[b * NB + j * 128:b * NB + (j + 1) * 128, :]) for j in range(NQT)]
        blocks.append(dict(id="p%d" % b, N=NB, xcol=XOFF + b * NB, segs=[(0, 0, NB)], tmt=tmt, x0=x0,
                           attn=[dict(q0=0, nq=NB, ktiles=prompt_ktiles([(j * 128, 128, b * NQT + j) for j in range(NQT)]))],
                           kdst=lambda l, seq, c0, n, t0=t0: k_p[l, t0 + c0:t0 + c0 + n, :],
                           vdst=lambda l, seq, c0, n, t0=t0: v_p[l, t0 + c0:t0 + c0 + n, :],
                           ydst=lambda seq, c0, n, b=b: y_p[b * NB + c0:b * NB + c0 + n, :]))

    def load_sample_cache(l, s):
        reg = s % 2
        kbase = NMETA + reg * PAST
        vbase = 1 + reg * NPT
        P.dma("pool", kstage[:], ck_d[l, s].rearrange("(t p) f -> p t f", p=128), writes=[kstb])
        for kt in range(NPT):
            P.dma("pool", VC[:, vbase + kt, :], cv_d[l, s, kt * 128:(kt + 1) * 128, :], writes=[ktb[vbase + kt]])
        for kt in range(NPT):
            for h in range(4):
                if h % 4 == 0:
                    pt, pb = getps()
                    ptb = pt[:].bitcast(BF16)
                tr(ptb[:, h * 128:(h + 1) * 128], kstage[:, kt, h * 128:(h + 1) * 128], identb[:], [kstb, cstb], [pb])
            for h in range(4):
                cp("dve" if kt % 2 else "act", KTC[:, h, kbase + kt * 128:kbase + (kt + 1) * 128], ptb[:, h * 128:(h + 1) * 128],
                   [pb], [ktb[vbase + kt]])

    pass
    STOP = int(os.environ.get("KSTOP", "99"))

    class _Stop(Exception):
        pass

    def stage(k):
        if STOP <= k:
            raise _Stop()
    for l in range(L):
        for _ in range(len(blocks)):
            wplan.extend((l, si, parts[0][3]) for si, parts in enumerate(block_wplan(l)))
    try:
      stage(1)
      convert_weights(0)
      stage(2)
      for l in range(L):
          if l + 1 < L:
              convert_weights(l + 1)
          load_states(l)
          stage(3)
          for s0 in range(min(2, NS)):
              load_sample_cache(l, s0)
          for bi, blk in enumerate(blocks):
              run_block(l, blk)
              stage(10 + bi)
          end_of_layer_states(l)
    except _Stop:
        pass
    P.wait_all("sp", out_bufs)
    P.drain("sp")
    P.build()
    return nc, P


_CACHE = {}
_DBG_HOOK = None


def _get_program(cfg_key):
    if cfg_key not in _CACHE:
        cfg = Cfg(*cfg_key)
        _CACHE[cfg_key] = (cfg,) + build_program(cfg)
    return _CACHE[cfg_key]


def kernel(x_prompt, x_sample, cache_k, cache_v, state_conv, state_hgrn, state_ffn,
           meta_tokens, rel_bias, g_mix, w_in, g_q, g_k, lam_q1, lam_k1, lam_q2, lam_k2, g_diff,
           conv_w, conv_b, ln_g, ln_b, lb_logits, g_hgrn, w_out, g_ffn, w_up, ffn_conv_w, ffn_conv_b,
           w_down, g_final, _nb=256):
    f = lambda a: np.ascontiguousarray(np.asarray(a, dtype=np.float32))
    x_prompt, x_sample, cache_k, cache_v = f(x_prompt), f(x_sample), f(cache_k), f(cache_v)
    state_conv, state_hgrn, state_ffn = f(state_conv), f(state_hgrn), f(state_ffn)
    L = w_in.shape[0]
    BP, SEQ = x_prompt.shape[0], x_prompt.shape[1]
    BS = x_sample.shape[0]
    PAST = cache_k.shape[2]
    NCORES = 8
    NS = BS // NCORES
    cfg, nc, P = _get_program((L, SEQ, PAST, NS, _nb))
    sidx, nrows, ngrp = svec_layout(L)
    sv = np.zeros((ngrp * 128, 128), np.float32)

    def put(key, arr):
        a = f(arr).reshape(-1, 128)
        sv[sidx[key]:sidx[key] + a.shape[0]] = a
    for l in range(L):
        put(("g_mix", l), g_mix[l]); put(("g_ffn", l), g_ffn[l]); put(("g_diff", l), g_diff[l])
        put(("conv_b", l), conv_b[l]); put(("ln_g", l), ln_g[l]); put(("ln_b", l), ln_b[l])
        put(("g_hgrn", l), g_hgrn[l]); put(("lbl", l), lb_logits[l]); put(("fcb", l), ffn_conv_b[l])
        for t in range(3):
            put(("fcw", l, t), ffn_conv_w[l, t])
        for t in range(CW):
            put(("cw", l, t), conv_w[l, t])
        put(("g_q", l), np.concatenate([f(g_q[l]), f(g_q[l])]))
        put(("g_k", l), np.concatenate([f(g_k[l]), f(g_k[l])]))
    put(("g_final",), g_final)
    lamv = np.stack([f(lam_q1).reshape(-1), f(lam_k1).reshape(-1), f(lam_q2).reshape(-1), f(lam_k2).reshape(-1)])
    consts = host_consts()
    shared = {"meta": f(meta_tokens), "w_in": f(w_in), "w_out": f(w_out), "w_up": f(w_up), "w_down": f(w_down),
              "svec": sv, "relb": f(rel_bias), "lamv": lamv}
    for k, v in consts.items():
        shared["c_" + k] = v
    in_maps = []
    for c in range(NCORES):
        m = dict(shared)
        m["xp"] = x_prompt[c % BP]
        sl = slice(c * NS, (c + 1) * NS)
        m["xs"] = x_sample[sl].reshape(NS * 16, D)
        m["ck"] = np.ascontiguousarray(cache_k[:, sl].reshape(L, NS, PAST, 512))
        m["cv"] = np.ascontiguousarray(cache_v[:, sl].reshape(L, NS, PAST, 512))
        m["sc"] = np.ascontiguousarray(state_conv[:, sl])
        m["sh"] = np.ascontiguousarray(state_hgrn[:, sl].reshape(L, NS, 256, 64))
        m["sf"] = np.ascontiguousarray(state_ffn[:, sl])
        in_maps.append(m)
    res = run_bass_kernel_spmd(nc, in_maps, core_ids=list(range(NCORES)))
    R = res.results
    TP = NMETA + SEQ
    y_prompt = np.stack([R[b]["y_p"] for b in range(BP)])
    y_sample = np.concatenate([R[c]["y_s"].reshape(NS, 16, D) for c in range(NCORES)])
    k_prompt = np.stack([R[b]["k_p"] for b in range(BP)], axis=1).reshape(L, BP, TP, 4, 2, 64)
    v_prompt = np.stack([R[b]["v_p"] for b in range(BP)], axis=1).reshape(L, BP, TP, 4, 128)
    conv_prompt = np.stack([R[b]["conv_p"] for b in range(BP)], axis=1)
    hgrn_prompt = np.stack([R[b]["hgrn_p"] for b in range(BP)], axis=1).reshape(L, BP, 4, 64, 64)
    ffn_prompt = np.stack([R[b]["ffn_p"] for b in range(BP)], axis=1)
    cat = lambda name: np.concatenate([R[c][name] for c in range(NCORES)], axis=1)
    k_sample = cat("k_s").reshape(L, BS, 16, 4, 2, 64)
    v_sample = cat("v_s").reshape(L, BS, 16, 4, 128)
    conv_sample = cat("conv_s")
    hgrn_sample = cat("hgrn_s").reshape(L, BS, 4, 64, 64)
    ffn_sample = cat("ffn_s")
    if _DBG_HOOK is not None:
        _DBG_HOOK(R)
    return (y_prompt, y_sample, k_prompt, v_prompt, conv_prompt, hgrn_prompt, ffn_prompt,
            k_sample, v_sample, conv_sample, hgrn_sample, ffn_sample)
```

```python
import math
import os
import numpy as np
import concourse.bass as bass
import concourse.mybir as mybir
from concourse.bass_utils import run_bass_kernel_spmd

F32 = mybir.dt.float32
BF16 = mybir.dt.bfloat16
AF = mybir.ActivationFunctionType
ALU = mybir.AluOpType
AX = mybir.AxisListType

EPOCH = 12000
DMA_EPOCH = 700
N_DMA_SEMS = 12


class Buf:
    __slots__ = ("name", "w", "r")

    def __init__(self, name=""):
        self.name = name
        self.w = None
        self.r = {}


class Prog:
    def __init__(self, nc):
        self.nc = nc
        self.eng_names = ("pe", "act", "dve", "pool", "sp")
        self.streams = {e: [] for e in self.eng_names}
        self.sem_handles = {}
        self.cur_sem = {}
        self.waited = {e: {} for e in self.eng_names}
        self.nsem = 0
        for e in self.eng_names:
            self._new_eng_sem(e)
        self.dma_sems = {q: [self._new_sem("d%s%d" % (q, i)) + [0] for i in range(N_DMA_SEMS)]
                         for q in ("sp", "pool", "act")}
        self.dma_rr = {q: 0 for q in ("sp", "pool", "act")}
        self.n_ops = {e: 0 for e in self.eng_names}
        self.n_waits = {e: 0 for e in self.eng_names}

    def _new_sem(self, name):
        self.nsem += 1
        key = "%s_%d" % (name, self.nsem)
        self.sem_handles[key] = self.nc.alloc_semaphore(key)
        return [key]

    def _new_eng_sem(self, e):
        key = self._new_sem("s" + e)[0]
        self.cur_sem[e] = [key, 0]

    def _collect(self, reads, writes):
        need = {}
        for b in reads:
            if b.w is not None:
                k, v = b.w
                if need.get(k, 0) < v:
                    need[k] = v
        for b in writes:
            if b.w is not None:
                k, v = b.w
                if need.get(k, 0) < v:
                    need[k] = v
            for k, v in b.r.items():
                if need.get(k, 0) < v:
                    need[k] = v
        return need

    def _waits_for(self, eng, need):
        ws = []
        wd = self.waited[eng]
        for k, v in need.items():
            if wd.get(k, 0) < v:
                wd[k] = v
                ws.append((k, v))
        return ws

    def _mark(self, tok, reads, writes):
        k, v = tok
        for b in reads:
            if b.r.get(k, 0) < v:
                b.r[k] = v
        for b in writes:
            b.w = tok
            b.r = {}

    enabled = True

    def section(self, k):
        import os
        if int(os.environ.get("PSTOP", "99")) <= k:
            self.enabled = False

    def bsection(self, k):
        import os
        if int(os.environ.get("BSTOP", "99")) <= k:
            self.enabled = False

    def op(self, eng, fn, reads=(), writes=()):
        if not self.enabled:
            return
        cs = self.cur_sem[eng]
        if cs[1] >= EPOCH:
            self._new_eng_sem(eng)
            cs = self.cur_sem[eng]
        need = self._collect(reads, writes)
        ws = self._waits_for(eng, need)
        if eng == "pe":
            ws = [w for w in ws if not w[0].startswith("spe_")]
        cs[1] += 1
        tok = (cs[0], cs[1])
        self.streams[eng].append((ws, fn, tok[0], 1))
        self._mark(tok, reads, writes)
        self.n_ops[eng] += 1
        self.n_waits[eng] += len(ws)
        return tok

    def dma(self, q, out_ap, in_ap, reads=(), writes=()):
        if not self.enabled:
            return
        lst = self.dma_sems[q]
        i = self.dma_rr[q]
        self.dma_rr[q] = (i + 1) % len(lst)
        ent = lst[i]
        if ent[1] >= DMA_EPOCH:
            ent = self._new_sem("d%s%d" % (q, i)) + [0]
            lst[i] = ent
        need = self._collect(reads, writes)
        if ent[1] > 0 and need.get(ent[0], 0) < ent[1] * 16:
            need[ent[0]] = ent[1] * 16
        ws = self._waits_for(q, need)
        ent[1] += 1
        tok = (ent[0], ent[1] * 16)

        def fn(e, out_ap=out_ap, in_ap=in_ap):
            return e.dma_start(out=out_ap, in_=in_ap)
        self.streams[q].append((ws, fn, tok[0], 16))
        self._mark(tok, reads, writes)
        self.n_ops[q] += 1
        self.n_waits[q] += len(ws)
        return tok

    def wait_all(self, eng, bufs):
        need = self._collect((), bufs)
        ws = self._waits_for(eng, need)
        if ws:
            self.streams[eng].append((ws, None, None, 0))

    def drain(self, eng="sp"):
        need = {}
        for e in self.eng_names:
            k, v = self.cur_sem[e]
            if v > 0:
                need[k] = v
        for q, lst in self.dma_sems.items():
            for ent in lst:
                if ent[1] > 0:
                    need[ent[0]] = ent[1] * 16
        ws = self._waits_for(eng, need)
        ws = [w for w in ws if w[0] != self.cur_sem[eng][0]]
        if ws:
            self.streams[eng].append((ws, None, None, 0))

    def build(self):
        nc = self.nc
        H = self.sem_handles

        def run(e, stream):
            for ws, fn, semkey, inc in stream:
                for k, v in ws:
                    e.wait_ge(H[k], v)
                if fn is not None:
                    fn(e).then_inc(H[semkey], inc)

        with nc.Block() as block:
            @block.tensor
            def _(e):
                run(e, self.streams["pe"])

            @block.scalar
            def _(e):
                run(e, self.streams["act"])

            @block.vector
            def _(e):
                run(e, self.streams["dve"])

            @block.gpsimd
            def _(e):
                run(e, self.streams["pool"])

            @block.sync
            def _(e):
                run(e, self.streams["sp"])


D = 1024
NMETA = 16
DFF = 2816
NFC = DFF // 128
CW = 31
INC = 3072
EPS = 1e-6
NEGM = -30000.0
SCALE = 0.125


class Cfg:
    def __init__(self, depth=4, seq=4096, past=1024, ns=4, nb=256):
        self.L = depth
        self.SEQ = seq
        self.PAST = past
        self.NS = ns
        self.NB = nb
        self.TP = NMETA + seq


def svec_layout(L):
    idx = {}
    n = 0

    def add(name, cnt):
        nonlocal n
        idx[name] = n
        n += cnt
    for l in range(L):
        add(("g_mix", l), 8)
        add(("g_ffn", l), 8)
        add(("g_diff", l), 4)
        add(("conv_b", l), 2)
        add(("ln_g", l), 2)
        add(("ln_b", l), 2)
        add(("g_hgrn", l), 2)
        add(("lbl", l), 2)
        add(("fcb", l), NFC)
        for t in range(3):
            add(("fcw", l, t), NFC)
        for t in range(CW):
            add(("cw", l, t), 2)
        add(("g_q", l), 1)
        add(("g_k", l), 1)
    add(("g_final",), 8)
    ngrp = (n + 127) // 128
    return idx, n, ngrp


def host_consts():
    c = {}
    c["ident"] = np.eye(128, dtype=np.float32)
    c["antiI"] = np.ascontiguousarray(np.eye(128, dtype=np.float32)[::-1])
    d = np.arange(512, dtype=np.int64) - 255
    nbk = 16
    me = 8
    ret = np.where(d > 0, nbk, 0)
    n = np.abs(d)
    nf = np.maximum(n, 1).astype(np.float32)
    large = me + (np.log(nf / np.float32(me)) / np.float32(math.log(128 / me)) * np.float32(nbk - me)).astype(np.int32)
    large = np.minimum(large, nbk - 1)
    bucket = ret + np.where(n < me, n, large)
    oh = np.zeros((32, 512), np.float32)
    oh[bucket[:511], np.arange(511)] = 1.0
    c["onehot"] = oh
    k = np.arange(128)[:, None]
    q = np.arange(128)[None, :]
    c["maskneg"] = np.where((k >= 64) & (q < 64), NEGM, 0.0).astype(np.float32)
    same = (k // 32) == (q // 32)
    c["tri"] = (same & (k <= q)).astype(np.float32)
    c["sa"] = (same & (k > q)).astype(np.float32)
    rm = np.ones((128, 1), np.float32)
    rm[64:96] = 0.0
    c["rowmask"] = rm
    return c


CONST_SHAPES = {"ident": [128, 128], "antiI": [128, 128], "onehot": [32, 512], "maskneg": [128, 128],
                "tri": [128, 128], "sa": [128, 128], "rowmask": [128, 1]}


def lam_init(l):
    return 0.8 - 0.6 * math.exp(-0.3 * l)


def build_program(cfg):
    L, SEQ, PAST, NS, NB, TP = cfg.L, cfg.SEQ, cfg.PAST, cfg.NS, cfg.NB, cfg.TP
    NPT = PAST // 128
    NT = max(1 + SEQ // 128, 1 + 2 * NPT + NS)
    KTW = max(TP, NMETA + 2 * PAST + 16 * NS)
    TT = TP + NS * 16
    nc = bass.Bass("TRN2", target_bir_lowering=False)
    P = Prog(nc)
    sidx, nrows, ngrp = svec_layout(L)

    def din(name, shape, dt=F32):
        return nc.dram_tensor(name, shape, dt, kind="ExternalInput").ap()

    def dout(name, shape):
        return nc.dram_tensor(name, shape, F32, kind="ExternalOutput").ap()

    def dscr(name, shape, dt):
        return nc.dram_tensor(name, shape, dt, kind="Internal").ap()

    xp_d = din("xp", [SEQ, D])
    meta_d = din("meta", [NMETA, D])
    xs_d = din("xs", [NS * 16, D])
    ck_d = din("ck", [L, NS, PAST, 512])
    cv_d = din("cv", [L, NS, PAST, 512])
    sc_d = din("sc", [L, NS, 30, 256])
    sh_d = din("sh", [L, NS, 256, 64])
    sf_d = din("sf", [L, NS, 2, DFF])
    w_in_d = din("w_in", [L, D, INC])
    w_out_d = din("w_out", [L, D, D])
    w_up_d = din("w_up", [L, D, 2 * DFF])
    w_dn_d = din("w_down", [L, DFF, D])
    svec_d = din("svec", [ngrp * 128, 128])
    relb_d = din("relb", [32, 4])
    lam_d = din("lamv", [4, L * 64])
    cst_d = {k: din("c_" + k, v) for k, v in CONST_SHAPES.items()}

    y_p = dout("y_p", [SEQ, D])
    y_s = dout("y_s", [NS * 16, D])
    k_p = dout("k_p", [L, TP, 512])
    v_p = dout("v_p", [L, TP, 512])
    conv_p = dout("conv_p", [L, 30, 256])
    hgrn_p = dout("hgrn_p", [L, 256, 64])
    ffn_p = dout("ffn_p", [L, 2, DFF])
    k_s = dout("k_s", [L, NS, 16, 512])
    v_s = dout("v_s", [L, NS, 16, 512])
    conv_s = dout("conv_s", [L, NS, 30, 256])
    hgrn_s = dout("hgrn_s", [L, NS, 256, 64])
    ffn_s = dout("ffn_s", [L, NS, 2, DFF])
    out_bufs = []
    DBG = os.environ.get("KDBG", "") != ""
    if DBG:
        dbg_mix = nc.dram_tensor("dbg_mix", [128, 8 * NB], BF16, kind="ExternalOutput").ap()
        dbg_x1 = dout("dbg_x1", [128, 8 * NB])
        dbg_h2 = nc.dram_tensor("dbg_h2", [128, 8 * NB], BF16, kind="ExternalOutput").ap()
        dbg_act = nc.dram_tensor("dbg_act", [128, NFC * NB], BF16, kind="ExternalOutput").ap()

    def obuf():
        b = Buf("out")
        out_bufs.append(b)
        return b

    NSL = 6 + 2 + NFC // 2 + 6
    wsl = dscr("wsl", [L, NSL, 128, 8 * 512], BF16)
    wbuf = [[Buf("wimg") for _ in range(NSL)] for l in range(L)]
    wb_in, wb_out, wb_up, wb_dn = w_in_d, w_out_d, w_up_d, w_dn_d
    xsc = dscr("xsc", [128, 8, TT], F32)
    xsc_b = {}
    tvd = dscr("tvd", [4, 512], F32)
    tvd_b = Buf("tvd")

    class TT_:
        pass

    def sb(name, shape, dt=F32):
        return nc.alloc_sbuf_tensor(name, shape, dt)

    class Rot:
        def __init__(self, name, n, shape, dt=F32):
            self.ts = [(sb("%s%d" % (name, i), shape, dt), Buf("%s%d" % (name, i))) for i in range(n)]
            self.i = 0

        def get(self):
            r = self.ts[self.i]
            self.i = (self.i + 1) % len(self.ts)
            return r

    ps = [nc.alloc_psum_tensor("ps%d" % i, [128, 512], F32) for i in range(8)]
    psb = [Buf("ps%d" % i) for i in range(8)]
    rot_ps_i = [0]

    def getps():
        i = 4 + rot_ps_i[0]
        rot_ps_i[0] = (rot_ps_i[0] + 1) % 4
        return ps[i], psb[i]

    def mm(out, lhsT, rhs, start, stop, R, W):
        P.op("pe", lambda e, o=out, l=lhsT, r=rhs, s=start, t=stop:
             e.matmul(o, lhsT=l, rhs=r, start=s, stop=t, skip_group_check=True), R, W)

    def tr(out, in_, ident, R, W):
        P.op("pe", lambda e, o=out, i=in_, d=ident: e.transpose(o, i, d), R, W)

    def act(out, in_, func, R, W, bias=None, scale=None):
        kw = {}
        if bias is not None:
            kw["bias"] = bias
        if scale is not None:
            kw["scale"] = scale
        P.op("act", lambda e, o=out, i=in_, f=func, kw=kw: e.activation(out=o, in_=i, func=f, **kw), R, W)

    def cp(eng, out, in_, R, W):
        if eng == "act":
            P.op("act", lambda e, o=out, i=in_: e.copy(out=o, in_=i), R, W)
        else:
            P.op(eng, lambda e, o=out, i=in_: e.tensor_copy(out=o, in_=i), R, W)

    def tt(eng, out, in0, in1, op, R, W):
        P.op(eng, lambda e, o=out, a=in0, b=in1, p=op: e.tensor_tensor(out=o, in0=a, in1=b, op=p), R, W)

    def ts(eng, out, in0, s1, s2, op0, op1, R, W):
        if s2 is None:
            P.op(eng, lambda e, o=out, a=in0, x=s1, p0=op0: e.tensor_scalar(out=o, in0=a, scalar1=x, scalar2=None, op0=p0), R, W)
        else:
            P.op(eng, lambda e, o=out, a=in0, x=s1, y=s2, p0=op0, p1=op1:
                 e.tensor_scalar(out=o, in0=a, scalar1=x, scalar2=y, op0=p0, op1=p1), R, W)

    def stt(eng, out, in0, scalar, in1, op0, op1, R, W):
        P.op(eng, lambda e, o=out, a=in0, s=scalar, b=in1, p0=op0, p1=op1:
             e.scalar_tensor_tensor(out=o, in0=a, scalar=s, in1=b, op0=p0, op1=p1), R, W)

    def mac(eng, acc, src, wcol, scratch, scratchb, R, W):
        if eng == "dve":
            stt("dve", acc, src, wcol, acc, ALU.mult, ALU.add, R + W, W)
        else:
            ts(eng, scratch, src, wcol, None, ALU.mult, None, R, [scratchb])
            tt(eng, acc, acc, scratch, ALU.add, [scratchb] + W, W)

    def memset(eng, ap, val, W):
        P.op(eng, lambda e, a=ap, v=val: e.memset(a, v), (), W)

    cst = {}
    cstb = Buf("consts")
    for k_, shp in CONST_SHAPES.items():
        cst[k_] = sb("sc_" + k_, shp)
        P.dma("sp", cst[k_][:], cst_d[k_][:, :], writes=[cstb])
    ident = cst["ident"]
    identb = sb("identb", [128, 128], BF16)
    cp("dve", identb[:], ident[:], [cstb], [cstb])
    ones_b = {}
    for nm, val in (("o1024", 1.0 / 1024), ("o128", 1.0 / 128), ("o256", 1.0 / 256), ("one", 1.0)):
        ones_b[nm] = sb("ones_" + nm, [128, 128], BF16)
        memset("pool", ones_b[nm][:], val, [cstb])
    blk64 = sb("blk64", [128, 128], BF16)
    memset("pool", blk64[:], 0.0, [cstb])
    memset("pool", blk64[0:64, 0:64], 1.0 / 64, [cstb])
    memset("pool", blk64[64:128, 64:128], 1.0 / 64, [cstb])

    P.section(1)
    vec = sb("vec", [128, ngrp * 128])
    vecb = Buf("vec")
    att = Rot("att", 2, [128, 128])
    svst = att
    xT = sb("xT", [128, 8, NB])
    xbc = [Buf("xT%d" % c) for c in range(8)]
    for g in range(ngrp):
        t_, b_ = svst.get()
        P.dma("sp", t_[:], svec_d[g * 128:(g + 1) * 128, :], writes=[b_])
        pt, pb = getps()
        tr(pt[:, 0:128], t_[:], ident[:], [b_, cstb], [pb])
        cp("dve", vec[:, g * 128:(g + 1) * 128], pt[:, 0:128], [pb], [vecb])

    def vcol(key, c=0):
        i = sidx[key] + c
        return vec[:, i:i + 1]

    for l in range(L):
        i = sidx[("g_diff", l)]
        ts("dve", vec[:, i:i + 4], vec[:, i:i + 4], 1.0 - lam_init(l), None, ALU.mult, None, [vecb], [vecb])

    P.section(2)
    lbT = sb("lbT", [128, 2, L])
    omlT = sb("omlT", [128, 2, L])
    lbtmp = sb("lbtmp", [128, 2, L])
    lbs = sb("lbs", [128, 2])
    for l in range(L):
        for c in range(2):
            act(lbtmp[:, c, l:l + 1], vcol(("lbl", l), c), AF.Exp, [vecb], [vecb])
    for c in range(2):
        P.op("dve", lambda e, c=c: e.reduce_sum(out=lbs[:, c:c + 1], in_=lbtmp[:, c, :], axis=AX.X), [vecb], [vecb])
    P.op("dve", lambda e: e.reciprocal(out=lbs[:], in_=lbs[:]), [vecb], [vecb])
    for c in range(2):
        ts("dve", lbtmp[:, c, :], lbtmp[:, c, :], lbs[:, c:c + 1], None, ALU.mult, None, [vecb], [vecb])
        memset("dve", lbT[:, c, 0:1], 0.0, [vecb])
        for l in range(1, L):
            tt("dve", lbT[:, c, l:l + 1], lbT[:, c, l - 1:l], lbtmp[:, c, l:l + 1], ALU.add, [vecb], [vecb])
    for c in range(2):
        ts("dve", omlT[:, c, :], lbT[:, c, :], -1.0, 1.0, ALU.mult, ALU.add, [vecb], [vecb])
    P.section(3)
    assert L * 64 <= NB
    lamt = xT[:, 0:4, 0:L * 64]
    neglam = sb("neglam", [128, L])
    lam2 = sb("lam2", [128, 2, L])
    for i in range(4):
        P.dma("sp", lamt[:, i, :], bass.AP(lam_d.tensor, i * L * 64, [[0, 128], [1, L * 64]]), writes=[vecb])
    tt("dve", lamt[:, 0, :], lamt[:, 0, :], lamt[:, 1, :], ALU.mult, [vecb], [vecb])
    tt("dve", lamt[:, 2, :], lamt[:, 2, :], lamt[:, 3, :], ALU.mult, [vecb], [vecb])
    for j, i in enumerate((0, 2)):
        for l in range(L):
            P.op("dve", lambda e, j=j, i=i, l=l: e.reduce_sum(out=lam2[:, j, l:l + 1], in_=lamt[:, i, l * 64:(l + 1) * 64], axis=AX.X),
                 [vecb], [vecb])
    act(lam2[:], lam2[:], AF.Exp, [vecb], [vecb])
    tt("dve", neglam[:], lam2[:, 1, :], lam2[:, 0, :], ALU.subtract, [vecb], [vecb])
    for l in range(L):
        ts("dve", neglam[:, l:l + 1], neglam[:, l:l + 1], -lam_init(l), None, ALU.add, None, [vecb], [vecb])

    P.section(4)
    relbc = sb("relbc", [128, 128])
    P.dma("sp", relbc[:], bass.AP(relb_d.tensor, 0, [[0, 128], [1, 128]]), writes=[vecb])
    relsb = sb("relsb", [32, 4])
    P.dma("sp", relsb[:], relb_d[:, :], writes=[vecb])
    pt, pb = getps()
    mm(pt[0:4, 0:512], relsb[:], cst["onehot"][:], True, True, [vecb, cstb], [pb])
    tvs = sb("tvs", [4, 512])
    cp("dve", tvs[:], pt[0:4, 0:512], [pb], [vecb])
    P.dma("sp", tvd[:, :], tvs[:], reads=[vecb], writes=[tvd_b])
    Tdiag = sb("Tdiag", [128, 4, 128])
    Tnear = sb("Tnear", [128, 4, 128])
    Tmeta = sb("Tmeta", [16, 4, 128])
    Tb = Buf("T")
    hank = att
    for h in range(4):
        for kind, base, nk in (("diag", 128, 128), ("near", 0, 128), ("meta", 112, 16)):
            ht, hb = hank.get()
            P.dma("sp", ht[:, 0:nk], bass.AP(tvd.tensor, h * 512 + base, [[1, 128], [1, nk]]), reads=[tvd_b], writes=[hb])
            pt, pb = getps()
            mm(pt[0:nk, 0:128], ht[:, 0:nk], cst["antiI"][:], True, True, [hb, cstb], [pb])
            if kind == "diag":
                tt("dve", Tdiag[:, h, :], pt[:, 0:128], cst["maskneg"][:], ALU.add, [pb, cstb], [Tb])
            elif kind == "near":
                cp("dve", Tnear[:, h, :], pt[:, 0:128], [pb], [Tb])
            else:
                cp("dve", Tmeta[:, h, :], pt[0:16, 0:128], [pb], [Tb])

    def cbias(h):
        return relbc[:, 15 * 4 + h:15 * 4 + h + 1]

    P.section(5)
    def convert_weights(l):
        for si, parts in enumerate(block_wplan(l)):
            for (l_, dr, k0, nk, c0, ncol, sc0) in parts:
                src = dr[l, k0 * 128:(k0 + nk) * 128, c0:c0 + ncol].rearrange("(kc p) c -> p kc c", p=128)
                dst = wsl[l, si].rearrange("p (kc c) -> p kc c", c=512)[:, 0:nk, sc0:sc0 + ncol]
                P.dma("pool", dst, src, reads=[], writes=[wbuf[l][si]])

    NSLOT = int(os.environ.get("NSLOT", "3"))
    WQ = os.environ.get("WQ", "sp").split(",")
    wslots = [(sb("wslot%d" % i, [128, 8, 512], BF16), Buf("wslot%d" % i)) for i in range(NSLOT)]
    wplan = []
    wstate = {"issued": 0, "used": 0}

    def block_wplan(l):
        pl = []
        for g in range(6):
            pl.append([(l, wb_in, 0, 8, g * 512, 512, 0)])
        for g in range(2):
            pl.append([(l, wb_out, 0, 8, g * 512, 512, 0)])
        for g in range(NFC // 2):
            pl.append([(l, wb_up, 0, 8, g * 256, 256, 0), (l, wb_up, 0, 8, DFF + g * 256, 256, 256)])
        for cg in range(2):
            for k0 in (0, 8, 16):
                pl.append([(l, wb_dn, k0, min(8, NFC - k0), cg * 512, 512, 0)])
        return pl

    def w_issue_upto(n):
        while wstate["issued"] < min(n, len(wplan)):
            i = wstate["issued"]
            st, sbuf_ = wslots[i % NSLOT]
            l, si, nkk = wplan[i]
            P.dma(WQ[i % len(WQ)], st[:].rearrange("p k c -> p (k c)")[:, 0:nkk * 512], wsl[l, si, :, 0:nkk * 512],
                  reads=[wbuf[l][si]], writes=[sbuf_])
            wstate["issued"] += 1

    def w_next():
        i = wstate["used"]
        w_issue_upto(i + NSLOT)
        wstate["used"] += 1
        return wslots[i % NSLOT]

    KTC = sb("KTC", [128, 4, KTW], BF16)
    VC = sb("VC", [128, NT, 512], BF16)
    ktb = [Buf("kt%d" % i) for i in range(NT)]
    hT = sb("hT", [128, 8, NB], BF16)
    hbc = [Buf("hT%d" % c) for c in range(8)]
    mixT = sb("mixT", [128, 8, NB], BF16)
    mixb = Buf("mixT")
    qT = sb("qT", [128, 4, 2 * NB], BF16)
    qb = [Buf("q%d" % h) for h in range(4)]
    for h_ in range(4):
        memset("pool", qT[:, h_, :], 0.0, [qb[h_]])
    sqr = Rot("sqr", 2, [128, NB], BF16)
    rsd = Rot("rsd", 2, [128, NB])
    tmp = Rot("tmp", 3, [128, NB])
    stg = Rot("stg", 2, [128, 512])
    Er = Rot("E", 3, [128, 2 * NB], BF16)
    ub = sb("ub", [128, 2, NB])
    ubb = Buf("ub")
    cacc = sb("cacc", [128, 2, NB])
    caccb = Buf("cacc")
    ubf = sb("ubf", [128, 2, NB], BF16)
    zfT = sb("zfT", [128, 2, NB])
    zfb = Buf("zfT")
    lfF = sb("lfF", [128, 2, NB])
    qcT = sb("qcT", [128, 2, NB])
    qcb = Buf("qcT")
    gsT = sb("gsT", [128, 2, NB])
    gsb = Buf("gsT")
    ebT = sb("ebT", [128, 2, NB])
    ebb = Buf("ebT")
    qtl = sb("qtl", [128, 2, NB])
    ktl = sb("ktl", [128, 2, NB])
    qkb = Buf("qk")
    NTM = max(1, NB // 128, 1 + NS)
    lfT = sb("lfT", [128, NTM, 256])
    omf = sb("omf", [128, NTM, 256])
    khz = sb("khz", [128, max(1, NB // 128), 256])
    vhT = sb("vhT", [128, NTM, 256])
    tmb = [Buf("tm%d" % i) for i in range(NTM)]
    NCH = NB // 32
    Sst = sb("Sst", [128, 2, NCH + 1, 64])
    Sb = Buf("S")
    ohT = sb("ohT", [128, 2, NB])
    ohb = Buf("ohT")
    bigbf = sb("bigbf", [128, max(NFC * NB, NPT * 512)], BF16)
    actT = bigbf[:, 0:NFC * NB].rearrange("p (c n) -> p c n", n=NB)
    actb = Buf("actT")
    gst = Rot("gst", 2, [128, 2 + NB])

    nseq = 1 + NS
    chalo = [sb("chalo%d" % i, [128, 2, 30]) for i in range(nseq)]
    fhalo = [sb("fhalo%d" % i, [128, NFC, 2]) for i in range(nseq)]
    Scur = [sb("Scur%d" % i, [128, 2, 64]) for i in range(nseq)]
    seqb = [Buf("seq%d" % i) for i in range(nseq)]

    kstage = bigbf[:, 0:NPT * 512].rearrange("p (t f) -> p t f", f=512)
    kstb = actb

    def norm_apply(src, N, ones_mat, gcol, outs, R, W, Rs=None, Ws=None):
        pt, pb = getps()
        n = len(src)
        for i, s in enumerate(src):
            sq, sqb = sqr.get()
            act(sq[:, 0:N], s, AF.Square, (Rs[i] if Rs else R), [sqb])
            mm(pt[:, 0:N], ones_mat[:], sq[:, 0:N], i == 0, i == n - 1, [sqb, cstb], [pb])
        rs, rsb = rsd.get()
        act(rs[:, 0:N], pt[:, 0:N], AF.Ln, [pb], [rsb], bias=EPS)
        act(rs[:, 0:N], rs[:, 0:N], AF.Exp, [rsb], [rsb], scale=-0.5)
        for i, s in enumerate(src):
            Ri = Rs[i] if Rs else R
            Wi = Ws[i] if Ws else W
            for o in outs[i]:
                if isinstance(o, tuple):
                    p0, p1, oap = o
                    stt("dve", oap, s[p0:p1], gcol[i][p0:p1], rs[p0:p1, 0:N], ALU.mult, ALU.mult, Ri + [rsb, vecb], Wi)
                else:
                    stt("dve", o, s, gcol[i], rs[:, 0:N], ALU.mult, ALU.mult, Ri + [rsb, vecb], Wi)

    def fm_chunk(wt, wtb, m, N, R):
        pt, pb = getps()
        for kc in range(8):
            mm(pt[:, 0:N], wt[:, kc, m * 128:(m + 1) * 128], hT[:, kc, 0:N], kc == 0, kc == 7, [wtb, hbc[kc]], [pb])
        return pt, pb

    def tm_proj(wt, wtb, c0, ntok, ncols, R):
        pt, pb = getps()
        for kc in range(8):
            mm(pt[0:ntok, 0:ncols], hT[:, kc, c0:c0 + ntok], wt[:, kc, 0:ncols], kc == 0, kc == 7, [wtb, hbc[kc]], [pb])
        return pt, pb

    def attention(l, N, q0, nq, ktiles, side=None, hold_last=False):
        flat = []
        for h in range(4):
            hs = []
            for kt in ktiles:
                vis = [b for b in kt["bias"] if b[2] != "skip"]
                if vis:
                    hs.append([h, kt, vis, False, False])
            hs[0][3] = True
            hs[-1][4] = True
            flat.extend(hs)
        LA = 2
        pend = {}

        def v3(ap2d, n):
            return ap2d.rearrange("p (c n) -> p c n", c=2) if n is None else ap2d

        def emit_st(i):
            h, kt, vis, _, _ = flat[i]
            v0 = vis[0][0]
            vn = nq - v0
            nk = kt["nk"]
            st_, stb = getps()
            qv = qT[:, h, :].rearrange("p (c n) -> p c n", c=2)[:, :, q0 + v0:q0 + v0 + vn]
            mm(st_[0:nk, 0:2 * vn], KTC[:, h, kt["kcol"]:kt["kcol"] + nk], qv,
               True, True, [kt["buf"], qb[h]], [stb])
            pend[i] = (st_, stb)

        defer = []

        def normalise(h, ob, zb):
            r0, r0b = tmp.get()
            act(r0[:, 0:nq], ps[zb][:, 0:nq], AF.Ln, [psb[zb]], [r0b])
            act(r0[:, 0:nq], r0[:, 0:nq], AF.Exp, [r0b], [r0b], scale=-1.0)
            tt("dve", r0[:, 0:nq], ps[ob][:, 0:nq], r0[:, 0:nq], ALU.mult, [psb[ob], r0b], [r0b])
            r1, r1b = tmp.get()
            act(r1[:, 0:nq], ps[zb][:, NB:NB + nq], AF.Ln, [psb[zb]], [r1b])
            act(r1[:, 0:nq], r1[:, 0:nq], AF.Exp, [r1b], [r1b], scale=-1.0)
            tt("dve", r1[:, 0:nq], ps[ob][:, NB:NB + nq], r1[:, 0:nq], ALU.mult, [psb[ob], r1b], [r1b])
            stt("dve", r0[:, 0:nq], r1[:, 0:nq], neglam[:, l:l + 1], r0[:, 0:nq], ALU.mult, ALU.add,
                [r0b, r1b, vecb], [r0b])
            norm_apply([r0[:, 0:nq]], nq, ones_b["o128"], [vcol(("g_diff", l), h)], [[mixT[:, h, q0:q0 + nq]]],
                       [r0b], [mixb])

        for i in range(min(LA, len(flat))):
            emit_st(i)
        for i, (h, kt, vis, first, last) in enumerate(flat):
            if i + LA < len(flat):
                emit_st(i + LA)
            if defer:
                defer[0][0] -= 1
                if defer[0][0] <= 0:
                    _, h_, ob_, zb_ = defer.pop(0)
                    normalise(h_, ob_, zb_)
            st_, stb = pend.pop(i)
            v0 = vis[0][0]
            vn = nq - v0
            nk = kt["nk"]
            ob, zb = 2 * (h % 2), 2 * (h % 2) + 1
            E, Eb = Er.get()
            sv = st_[0:nk, 0:2 * vn].rearrange("p (c n) -> p c n", c=2)
            Ev = E[0:nk, :].rearrange("p (c n) -> p c n", c=2)[:, :, 0:vn]
            for (c0, n, kind) in vis:
                o = c0 - v0
                if kind == "const":
                    act(Ev[:, :, o:o + n], sv[:, :, o:o + n], AF.Exp, [vecb], [Eb, stb],
                        bias=cbias(h)[0:nk, :], scale=SCALE)
                else:
                    Tm = {"diag": Tdiag, "near": Tnear, "meta": Tmeta}[kind]
                    t_, tb_ = tmp.get()
                    for c in range(2):
                        stt("dve", t_[0:nk, c * n:(c + 1) * n], st_[0:nk, c * vn + o:c * vn + o + n], SCALE, Tm[0:nk, h, 0:n],
                            ALU.mult, ALU.add, [Tb], [tb_, stb])
                        act(E[0:nk, c * NB + o:c * NB + o + n], t_[0:nk, c * n:(c + 1) * n], AF.Exp, [tb_], [Eb])
            if vn == NB:
                mm(ps[ob][:, :], VC[0:nk, kt["vt"], h * 128:(h + 1) * 128], E[0:nk, :], first, False, [kt["buf"], Eb], [psb[ob]])
                mm(ps[zb][:, :], ones_b["one"][0:nk, :], E[0:nk, :], first, False, [Eb, cstb], [psb[zb]])
            else:
                for c in range(2):
                    mm(ps[ob][:, c * NB + v0:c * NB + v0 + vn], VC[0:nk, kt["vt"], h * 128:(h + 1) * 128],
                       E[0:nk, c * NB:c * NB + vn], first and c == 0, False, [kt["buf"], Eb], [psb[ob]])
                    mm(ps[zb][:, c * NB + v0:c * NB + v0 + vn], ones_b["one"][0:nk, :],
                       E[0:nk, c * NB:c * NB + vn], first and c == 0, False, [Eb, cstb], [psb[zb]])
            if last:
                while defer:
                    _, h_, ob_, zb_ = defer.pop(0)
                    normalise(h_, ob_, zb_)
                defer.append([5, h, ob, zb])
            if side is not None:
                next(side, None)
                next(side, None)
        def flush():
            while defer:
                _, h_, ob_, zb_ = defer.pop(0)
                normalise(h_, ob_, zb_)
        if not hold_last:
            flush()
        return flush

    def conv_taps_gen(l, N, segs):
        for (seq, c0, n) in segs:
            for c in range(2):
                win, winb = cwin.get()
                cp("pool", win[:, 0:30], chalo[seq][:, c, :], [seqb[seq]], [winb])
                cp("pool", win[:, 30:30 + n], ub[:, c, c0:c0 + n], [ubb], [winb])
                cp("pool", chalo[seq][:, c, :], win[:, n:n + 30], [winb], [seqb[seq]])
                ts("dve", cacc[:, c, c0:c0 + n], win[:, 0:n], vcol(("cw", l, 0), c), vcol(("conv_b", l), c),
                   ALU.mult, ALU.add, [winb, vecb], [caccb])
                yield
                for j in range(1, CW):
                    mac("dve", cacc[:, c, c0:c0 + n], win[:, j:j + n], vcol(("cw", l, j), c), None, None,
                        [winb, vecb], [caccb])
                    yield

    def conv_ln(l, N):
        pt, pb = getps()
        for c in range(2):
            cp("act", ubf[:, c, 0:N], cacc[:, c, 0:N], [caccb], [ubb])
            mm(pt[:, 0:N], ones_b["o256"][:], ubf[:, c, 0:N], c == 0, c == 1, [ubb, cstb], [pb])
        for c in range(2):
            tt("dve", cacc[:, c, 0:N], cacc[:, c, 0:N], pt[:, 0:N], ALU.subtract, [caccb, pb], [caccb])
        norm_apply([cacc[:, 0, 0:N], cacc[:, 1, 0:N]], N, ones_b["o256"],
                   [vcol(("ln_g", l), 0), vcol(("ln_g", l), 1)],
                   [[cacc[:, 0, 0:N]], [cacc[:, 1, 0:N]]], [caccb], [caccb])
        for c in range(2):
            act(mixT[:, 4 + c, 0:N], cacc[:, c, 0:N], AF.Silu, [caccb, vecb], [mixb], bias=vcol(("ln_b", l), c))

    cwin = Rot("cwin", 2, [128, 30 + NB])

    def hgrn(l, N, segs, tmtiles):
        for (j, c0, n, seq) in tmtiles:
            for c in range(2):
                pt, pb = getps()
                mm(pt[:, 0:n], lfT[0:n, j, c * 128:(c + 1) * 128], cst["tri"][0:n, 0:n], True, True, [tmb[j], cstb], [pb])
                act(ebT[:, c, c0:c0 + n], pt[:, 0:n], AF.Exp, [pb], [ebb])
                t_, tb_ = tmp.get()
                act(t_[:, 0:n], pt[:, 0:n], AF.Exp, [pb], [tb_], scale=-1.0)
                tt("dve", ktl[:, c, c0:c0 + n], zfT[:, c, c0:c0 + n], t_[:, 0:n], ALU.mult, [zfb, tb_], [qkb])
                tt("dve", qtl[:, c, c0:c0 + n], qcT[:, c, c0:c0 + n], ebT[:, c, c0:c0 + n], ALU.mult, [qcb, ebb], [qkb])
            pt, pb = getps()
            mm(pt[0:n, 0:256], cst["sa"][0:n, 0:n], lfT[0:n, j, :], True, True, [tmb[j], cstb], [pb])
            t_, tb_ = tmp.get()
            act(t_[0:n, 0:256], pt[0:n, 0:256], AF.Exp, [pb], [tb_])
            tt("dve", omf[0:n, j, :], omf[0:n, j, :], t_[0:n, 0:256], ALU.mult, [tmb[j], tb_], [tmb[j]])
            if n == 128:
                ts("dve", khz[64:128, j, :], omf[64:128, j, :], cst["rowmask"][64:128, 0:1], None, ALU.mult, None,
                   [tmb[j], cstb], [tmb[j]])
        def tmap(t):
            for (j_, c0_, n_, s_) in tmtiles:
                if c0_ <= t < c0_ + n_:
                    return j_, t - c0_
            raise AssertionError
        for (seq, c0, n) in segs:
            nch = (n + 31) // 32
            for p in range(2):
                cp("dve", Sst[:, p, 0, :], Scur[seq][:, p, :], [seqb[seq]], [Sb])
            for ch in range(nch):
                t0 = c0 + ch * 32
                nt = min(32, n - ch * 32)
                j, r0 = tmap(t0)
                pt, pb = getps()
                for hh in range(4):
                    p, lo = hh // 2, 64 * (hh % 2)
                    if r0 == 96:
                        lhs = khz[64:128, j, hh * 64:(hh + 1) * 64]
                        rhs = vhT[64:128, j, hh * 64:(hh + 1) * 64]
                    else:
                        lhs = omf[r0:r0 + nt, j, hh * 64:(hh + 1) * 64]
                        rhs = vhT[r0:r0 + nt, j, hh * 64:(hh + 1) * 64]
                    mm(pt[lo:lo + 64, p * 64:(p + 1) * 64], lhs, rhs, True, True, [tmb[j]], [pb])
                for p in range(2):
                    stt("dve", Sst[:, p, ch + 1, :], Sst[:, p, ch, :], ebT[:, p, t0 + nt - 1:t0 + nt],
                        pt[:, p * 64:(p + 1) * 64], ALU.mult, ALU.add, [Sb, ebb, pb], [Sb])
            for p in range(2):
                cp("dve", Scur[seq][:, p, :], Sst[:, p, nch, :], [Sb], [seqb[seq]])
            for t0 in range(c0, c0 + n, 128):
                nt = min(128, c0 + n - t0)
                j, r0 = tmap(t0)
                ot, otb = ps[0], psb[0]
                for hh in range(4):
                    p, lo = hh // 2, 64 * (hh % 2)
                    pt, pb = getps()
                    mm(pt[r0:r0 + nt, 0:nt], ktl[lo:lo + 64, p, t0:t0 + nt], qtl[lo:lo + 64, p, t0:t0 + nt], True, True,
                       [qkb], [pb])
                    a_, ab_ = att.get()
                    tt("dve", a_[r0:r0 + nt, 0:nt], pt[r0:r0 + nt, 0:nt], cst["tri"][r0:r0 + nt, r0:r0 + nt], ALU.mult,
                       [pb, cstb], [ab_])
                    mm(ot[lo:lo + 64, p * 128:p * 128 + nt], vhT[r0:r0 + nt, j, hh * 64:(hh + 1) * 64], a_[r0:r0 + nt, 0:nt],
                       True, False, [tmb[j], ab_], [otb])
                    for ch in range((nt + 31) // 32):
                        cc = (t0 - c0) // 32 + ch
                        n32 = min(32, nt - ch * 32)
                        mm(ot[lo:lo + 64, p * 128 + ch * 32:p * 128 + ch * 32 + n32], Sst[lo:lo + 64, p, cc, :],
                           qtl[lo:lo + 64, p, t0 + ch * 32:t0 + ch * 32 + n32], False, False, [Sb, qkb], [otb])
                for p in range(2):
                    cp("act", ohT[:, p, t0:t0 + nt], ot[:, p * 128:p * 128 + nt], [otb], [ohb])
        for p in range(2):
            norm_apply([ohT[:, p, 0:N]], N, blk64, [vcol(("g_hgrn", l), p)], [[ohT[:, p, 0:N]]], [ohb], [ohb])
            tt("dve", mixT[:, 6 + p, 0:N], ohT[:, p, 0:N], gsT[:, p, 0:N], ALU.mult, [ohb, gsb], [mixb])

    def tm_out(srcs, ntok, dst_ap, R):
        for g0 in range(0, len(srcs), 4):
            grp = srcs[g0:g0 + 4]
            pt, pb = getps()
            for i, s in enumerate(grp):
                tr(pt[0:ntok, i * 128:(i + 1) * 128], s, ident[:], R + [cstb], [pb])
            st_, stb = stg.get()
            cp("act", st_[0:ntok, 0:128 * len(grp)], pt[0:ntok, 0:128 * len(grp)], [pb], [stb])
            P.dma("sp", dst_ap[:, g0 * 128:g0 * 128 + 128 * len(grp)], st_[0:ntok, 0:128 * len(grp)], reads=[stb], writes=[obuf()])

    def fm_in(src_ap, ntok, nchunks, dst_fn, W, q="sp", Wfn=None):
        for g0 in range(0, nchunks, 4):
            ng = min(4, nchunks - g0)
            xi, xib = stg.get()
            P.dma(q, xi[0:ntok, 0:128 * ng], src_ap[:, g0 * 128:(g0 + ng) * 128], writes=[xib])
            pt, pb = getps()
            for i in range(ng):
                tr(pt[:, i * 128:i * 128 + ntok], xi[0:ntok, i * 128:(i + 1) * 128], ident[0:ntok, 0:ntok],
                   [xib, cstb], [pb])
            for i in range(ng):
                cp("dve", dst_fn(g0 + i), pt[:, i * 128:i * 128 + ntok], [pb], (Wfn(g0 + i) if Wfn else W))

    def run_block(l, blk):
        N = blk["N"]
        xc0 = blk["xcol"]
        segs = blk["segs"]
        tmt = blk["tmt"]
        key = (blk["id"],)
        if key not in xsc_b:
            xsc_b[key] = [Buf("xscA"), Buf("xscB")]
        if l == 0:
            for (j, c0, n, seq, vt, kcol, src) in blk["x0"]:
                fm_in(src, n, 8, lambda c, c0=c0, n=n: xT[:, c, c0:c0 + n], None, Wfn=lambda c: [xbc[c]])
        else:
            for hf in range(2):
                P.dma("sp", xT[:, 4 * hf:4 * hf + 4, 0:N], xsc[:, 4 * hf:4 * hf + 4, xc0:xc0 + N],
                      reads=[xsc_b[key][hf]], writes=xbc[4 * hf:4 * hf + 4])
        P.bsection(1)
        norm_apply([xT[:, c, 0:N] for c in range(8)], N, ones_b["o1024"], [vcol(("g_mix", l), c) for c in range(8)],
                   [[hT[:, c, 0:N]] for c in range(8)], None, None,
                   Rs=[[xbc[c]] for c in range(8)], Ws=[[hbc[c]] for c in range(8)])
        P.bsection(2)
        wt, wtb = w_next()
        for h in range(4):
            pt, pb = fm_chunk(wt, wtb, h, N, [])
            norm_apply([pt[:, 0:N]], N, blk64, [vcol(("g_q", l))],
                       [[(0, 64, qT[0:64, h, 0:N]), (64, 128, qT[64:128, h, NB:NB + N])]], [pb], [qb[h]])
        P.bsection(3)
        wt, wtb = w_next()
        kn_l = []
        for h in range(4):
            pt, pb = fm_chunk(wt, wtb, h, N, [])
            kn, knb = knr.get()
            outs = [kn[:, 0:N]]
            norm_apply([pt[:, 0:N]], N, blk64, [vcol(("g_k", l))], [outs], [pb], [knb])
            for (j, c0, n, seq, vt, kcol, *_) in tmt:
                cp("pool", KTC[:, h, kcol:kcol + n], kn[:, c0:c0 + n], [knb], [ktb[vt]])
            kn_l.append((kn, knb))
        P.bsection(4)
        wt, wtb = w_next()
        for (j, c0, n, seq, vt, kcol, *_) in tmt:
            VSK = os.environ.get("VSKIP", "")
            pt, pb = tm_proj(wt, wtb, c0, n, 512, []) if "m" not in VSK else getps()
            st_, stb = stg.get()
            if "a" not in VSK:
                cp("act", st_[0:n, :], pt[0:n, :], [pb], [stb])
            if "d" not in VSK:
                cp("dve", VC[0:n, vt, :], st_[0:n, :], [stb], [ktb[vt]])
            if "o" not in VSK:
                P.dma("sp", blk["vdst"](l, seq, c0, n), st_[0:n, :], reads=[stb], writes=[obuf()])
        P.bsection(5)
        wt, wtb = w_next()
        pga = [fm_chunk(wt, wtb, m, N, []) for m in range(2)]
        for c in range(2):
            pt, pb = fm_chunk(wt, wtb, 2 + c, N, [])
            t_, tb_ = tmp.get()
            act(t_[:, 0:N], pt[:, 0:N], AF.Sigmoid, [pb], [tb_])
            tt("dve", ub[:, c, 0:N], pga[c][0][:, 0:N], t_[:, 0:N], ALU.mult, [pga[c][1], tb_], [ubb])
        P.bsection(6)
        wt, wtb = w_next()
        for c in range(2):
            pt, pb = fm_chunk(wt, wtb, c, N, [])
            act(zfT[:, c, 0:N], pt[:, 0:N], AF.Sigmoid, [pb], [zfb])
            ts("dve", zfT[:, c, 0:N], zfT[:, c, 0:N], omlT[:, c, l:l + 1], lbT[:, c, l:l + 1], ALU.mult, ALU.add,
               [zfb, vecb], [zfb])
            act(lfF[:, c, 0:N], zfT[:, c, 0:N], AF.Ln, [zfb], [zfb])
            ts("dve", zfT[:, c, 0:N], zfT[:, c, 0:N], -1.0, 1.0, ALU.mult, ALU.add, [zfb], [zfb])
        for (j, c0, n, seq, vt, kcol, *_) in tmt:
            pt, pb = getps()
            for c in range(2):
                tr(pt[0:n, c * 128:(c + 1) * 128], lfF[:, c, c0:c0 + n], ident[:], [zfb, cstb], [pb])
                tr(pt[0:n, 256 + c * 128:256 + (c + 1) * 128], zfT[:, c, c0:c0 + n], ident[:], [zfb, cstb], [pb])
            cp("act", lfT[0:n, j, :], pt[0:n, 0:256], [pb], [tmb[j]])
            cp("act", omf[0:n, j, :], pt[0:n, 256:512], [pb], [tmb[j]])
            pt, pb = getps()
            for kc in range(8):
                mm(pt[0:n, 0:256], hT[:, kc, c0:c0 + n], wt[:, kc, 256:512], kc == 0, kc == 7, [wtb, hbc[kc]], [pb])
            cp("act", vhT[0:n, j, :], pt[0:n, 0:256], [pb], [tmb[j]])
        P.bsection(7)
        wt, wtb = w_next()
        for c in range(2):
            pt, pb = fm_chunk(wt, wtb, c, N, [])
            cp("act", qcT[:, c, 0:N], pt[:, 0:N], [pb], [qcb])
        for c in range(2):
            pt, pb = fm_chunk(wt, wtb, 2 + c, N, [])
            act(gsT[:, c, 0:N], pt[:, 0:N], AF.Silu, [pb], [gsb])
        for (j, c0, n, seq, vt, kcol, *_) in tmt:
            dst = blk["kdst"](l, seq, c0, n)
            tm_out([kn[:, c0:c0 + n] for (kn, knb) in kn_l], n, dst, [knb for (kn, knb) in kn_l])
        P.bsection(8)
        side = conv_taps_gen(l, N, segs)
        late_flush = None
        for ai, a in enumerate(blk["attn"]):
            fl = attention(l, N, a["q0"], a["nq"], a["ktiles"](l), side, hold_last=(ai == len(blk["attn"]) - 1))
            if ai == len(blk["attn"]) - 1:
                late_flush = fl
            if a.get("post"):
                a["post"](l)
        for _ in side:
            pass
        P.bsection(9)
        conv_ln(l, N)
        P.bsection(10)
        hgrn(l, N, segs, [(j, c0, n, seq) for (j, c0, n, seq, *_) in tmt])
        if late_flush is not None:
            late_flush()
        P.bsection(11)
        for g in range(2):
            wt, wtb = w_next()
            for m in range(4):
                pt, pb = getps()
                for kc in range(8):
                    mm(pt[:, 0:N], wt[:, kc, m * 128:(m + 1) * 128], mixT[:, kc, 0:N], kc == 0, kc == 7, [wtb, mixb], [pb])
                tt("dve", xT[:, g * 4 + m, 0:N], xT[:, g * 4 + m, 0:N], pt[:, 0:N], ALU.add, [xbc[g * 4 + m], pb], [xbc[g * 4 + m]])
        if DBG and blk["id"] == "pm" and l == 0:
            P.dma("sp", dbg_mix.rearrange("p (c n) -> p c n", n=NB), mixT[:], reads=[mixb], writes=[obuf()])
        P.bsection(12)
        norm_apply([xT[:, c, 0:N] for c in range(8)], N, ones_b["o1024"], [vcol(("g_ffn", l), c) for c in range(8)],
                   [[hT[:, c, 0:N]] for c in range(8)], None, None,
                   Rs=[[xbc[c]] for c in range(8)], Ws=[[hbc[c]] for c in range(8)])
        P.bsection(13)
        for g in range(NFC // 2):
            wg, wgb = w_next()
            for m in range(2):
                fc = g * 2 + m
                pg, pgb = fm_chunk(wg, wgb, m, N, [])
                pv, pvb = fm_chunk(wg, wgb, 2 + m, N, [])
                gs_, gsb_ = gst.get()
                t_, tb_ = tmp.get()
                for si, (seq, c0, n) in enumerate(segs):
                    o = si * (2 + n)
                    cp("pool", gs_[:, o:o + 2], fhalo[seq][:, fc, :], [seqb[seq]], [gsb_])
                    cp("act", gs_[:, o + 2:o + 2 + n], pg[:, c0:c0 + n], [pgb], [gsb_])
                    act(t_[:, c0:c0 + n], pg[:, c0:c0 + n], AF.Identity, [pgb, vecb], [tb_],
                        bias=vcol(("fcb", l), fc), scale=vcol(("fcw", l, 2), fc))
                    for tap in (1, 0):
                        mac("dve", t_[:, c0:c0 + n], gs_[:, o + tap:o + tap + n], vcol(("fcw", l, tap), fc), None, None,
                            [gsb_, vecb], [tb_])
                    cp("act", fhalo[seq][:, fc, :], pg[:, c0 + n - 2:c0 + n], [pgb, gsb_], [seqb[seq]])
                act(t_[:, 0:N], t_[:, 0:N], AF.Gelu_apprx_tanh, [tb_], [tb_])
                tt("dve", actT[:, fc, 0:N], t_[:, 0:N], pv[:, 0:N], ALU.mult, [tb_, pvb], [actb])
        if DBG and blk["id"] == "pm" and l == 0:
            P.dma("sp", dbg_h2.rearrange("p (c n) -> p c n", n=NB), hT[:], reads=list(hbc), writes=[obuf()])
            P.dma("sp", dbg_act.rearrange("p (c n) -> p c n", n=NB), actT, reads=[actb], writes=[obuf()])
        P.bsection(14)
        for cg in range(2):
            for ki, k0 in enumerate((0, 8, 16)):
                wt, wtb = w_next()
                nk = min(8, NFC - k0)
                for m in range(4):
                    for kc in range(nk):
                        mm(ps[m][:, 0:N], wt[:, kc, m * 128:(m + 1) * 128], actT[:, k0 + kc, 0:N],
                           ki == 0 and kc == 0, False, [wtb, actb], [psb[m]])
            for m in range(4):
                tt("dve", xT[:, cg * 4 + m, 0:N], xT[:, cg * 4 + m, 0:N], ps[m][:, 0:N], ALU.add, [xbc[cg * 4 + m], psb[m]], [xbc[cg * 4 + m]])
        if DBG and blk["id"] == "pm" and l == 0:
            P.dma("sp", dbg_x1.rearrange("p (c n) -> p c n", n=NB), xT[:], reads=list(xbc), writes=[obuf()])
        P.bsection(15)
        if l < L - 1:
            for hf in range(2):
                P.dma("sp", xsc[:, 4 * hf:4 * hf + 4, xc0:xc0 + N], xT[:, 4 * hf:4 * hf + 4, 0:N],
                      reads=xbc[4 * hf:4 * hf + 4], writes=[xsc_b[key][hf]])
        else:
            if blk["ydst"] is not None:
                gf = sidx[("g_final",)]
                norm_apply([xT[:, c, 0:N] for c in range(8)], N, ones_b["o1024"], [vec[:, gf + c:gf + c + 1] for c in range(8)],
                           [[xT[:, c, 0:N]] for c in range(8)], None, None,
                           Rs=[[xbc[c]] for c in range(8)], Ws=[[xbc[c]] for c in range(8)])
                for (j, c0, n, seq, *_) in tmt:
                    ydst_ = blk["ydst"](seq, c0, n)
                    if ydst_ is not None:
                        tm_out([xT[:, c, c0:c0 + n] for c in range(8)], n, ydst_, list(xbc))

    knr = Rot("kn", 4, [128, NB])

    def end_of_layer_states(l):
        for seq in range(nseq):
            if seq == 0:
                cdst, hdst, fdst = conv_p[l], hgrn_p[l], ffn_p[l]
            else:
                cdst, hdst, fdst = conv_s[l, seq - 1], hgrn_s[l, seq - 1], ffn_s[l, seq - 1]
            tm_out([chalo[seq][:, c, :] for c in range(2)], 30, cdst, [seqb[seq]])
            for p in range(2):
                P.dma("sp", hdst[p * 128:(p + 1) * 128, :], Scur[seq][:, p, :], reads=[seqb[seq]], writes=[obuf()])
            tm_out([fhalo[seq][:, fc, :] for fc in range(NFC)], 2, fdst, [seqb[seq]])

    def load_states(l):
        for seq in range(nseq):
            if seq == 0:
                memset("pool", chalo[0][:], 0.0, [seqb[0]])
                memset("pool", fhalo[0][:], 0.0, [seqb[0]])
                memset("pool", Scur[0][:], 0.0, [seqb[0]])
            else:
                s = seq - 1
                fm_in(sc_d[l, s], 30, 2, lambda c, seq=seq: chalo[seq][:, c, :], [seqb[seq]])
                for p in range(2):
                    P.dma("sp", Scur[seq][:, p, :], sh_d[l, s, p * 128:(p + 1) * 128, :], writes=[seqb[seq]])
                for g0 in range(0, NFC, 8):
                    ng = min(8, NFC - g0)
                    fm_in(sf_d[l, s, :, g0 * 128:(g0 + ng) * 128], 2, ng,
                          lambda c, seq=seq, g0=g0: fhalo[seq][:, g0 + c, :], [seqb[seq]])

    blocks = []
    NQT = NB // 128

    def prompt_ktiles(qtiles):
        def f(l):
            kts = []
            if qtiles[0][2] < 0:
                kts.append(dict(kcol=0, nk=16, vt=0, buf=ktb[0], bias=[(0, 16, "diag")]))
                return kts
            kts.append(dict(kcol=0, nk=16, vt=0, buf=ktb[0],
                            bias=[(c0, n, "meta" if gi == 0 else "const") for (c0, n, gi) in qtiles]))
            last = qtiles[-1][2]
            for kt in range(last + 1):
                bias = []
                for (c0, n, gi) in qtiles:
                    if kt > gi:
                        kind = "skip"
                    elif kt == gi:
                        kind = "diag"
                    elif kt == gi - 1:
                        kind = "near"
                    else:
                        kind = "const"
                    bias.append((c0, n, kind))
                kts.append(dict(kcol=NMETA + kt * 128, nk=128, vt=1 + kt, buf=ktb[1 + kt], bias=bias))
            return kts
        return f

    XOFF = NMETA + 16 * NS

    def sample_ktiles(s):
        reg = s % 2
        kbase = NMETA + reg * PAST
        vbase = 1 + reg * NPT

        def f(l):
            kts = []
            for kt in range(NPT):
                kind = "near" if kt == NPT - 1 else "const"
                kts.append(dict(kcol=kbase + kt * 128, nk=128, vt=vbase + kt, buf=ktb[vbase + kt], bias=[(0, 16, kind)]))
            kts.append(dict(kcol=NMETA + 2 * PAST + 16 * s, nk=16, vt=1 + 2 * NPT + s, buf=ktb[1 + 2 * NPT + s],
                            bias=[(0, 16, "diag")]))
            return kts
        return f

    f_tmt = [(0, 0, NMETA, 0, 0, 0)]
    f_x0 = [(0, 0, NMETA, 0, 0, 0, meta_d[:, :])]
    for s in range(NS):
        c0s = NMETA + 16 * s
        f_tmt.append((1 + s, c0s, 16, 1 + s, 1 + 2 * NPT + s, NMETA + 2 * PAST + 16 * s))
        f_x0.append((1 + s, c0s, 16, 1 + s, 0, 0, xs_d[s * 16:(s + 1) * 16, :]))

    def mk_sattn(s):
        def post(l):
            if s + 2 < NS:
                load_sample_cache(l, s + 2)
        return dict(q0=NMETA + 16 * s, nq=16, ktiles=sample_ktiles(s), post=post)

    blocks.append(dict(id="pm", N=XOFF, xcol=0, segs=[(0, 0, NMETA)] + [(1 + s, NMETA + 16 * s, 16) for s in range(NS)],
                       tmt=f_tmt, x0=f_x0,
                       attn=[dict(q0=0, nq=NMETA, ktiles=prompt_ktiles([(0, NMETA, -1)]))] + [mk_sattn(s) for s in range(NS)],
                       kdst=lambda l, seq, c0, n: (k_p[l, 0:NMETA, :] if seq == 0 else k_s[l, seq - 1]),
                       vdst=lambda l, seq, c0, n: (v_p[l, 0:NMETA, :] if seq == 0 else v_s[l, seq - 1]),
                       ydst=lambda seq, c0, n: (None if seq == 0 else y_s[(seq - 1) * 16:seq * 16, :])))
    for b in range(SEQ // NB):
        t0 = NMETA + b * NB
        tmt = [(j, j * 128, 128, 0, 1 + b * NQT + j, t0 + j * 128) for j in range(NQT)]
        x0 = [(j, j * 128, 128, 0, 0, 0, xp_d[b * NB + j * 128:b * NB + (j + 1) * 128, :]) for j in range(NQT)]
        blocks.append(dict(id="p%d" % b, N=NB, xcol=XOFF + b * NB, segs=[(0, 0, NB)], tmt=tmt, x0=x0,
                           attn=[dict(q0=0, nq=NB, ktiles=prompt_ktiles([(j * 128, 128, b * NQT + j) for j in range(NQT)]))],
                           kdst=lambda l, seq, c0, n, t0=t0: k_p[l, t0 + c0:t0 + c0 + n, :],
                           vdst=lambda l, seq, c0, n, t0=t0: v_p[l, t0 + c0:t0 + c0 + n, :],
                           ydst=lambda seq, c0, n, b=b: y_p[b * NB + c0:b * NB + c0 + n, :]))

    def load_sample_cache(l, s):
        reg = s % 2
        kbase = NMETA + reg * PAST
        vbase = 1 + reg * NPT
        P.dma("pool", kstage[:], ck_d[l, s].rearrange("(t p) f -> p t f", p=128), writes=[kstb])
        for kt in range(NPT):
            P.dma("pool", VC[:, vbase + kt, :], cv_d[l, s, kt * 128:(kt + 1) * 128, :], writes=[ktb[vbase + kt]])
        for kt in range(NPT):
            for h in range(4):
                if h % 4 == 0:
                    pt, pb = getps()
                    ptb = pt[:].bitcast(BF16)
                tr(ptb[:, h * 128:(h + 1) * 128], kstage[:, kt, h * 128:(h + 1) * 128], identb[:], [kstb, cstb], [pb])
            for h in range(4):
                cp("dve" if kt % 2 else "act", KTC[:, h, kbase + kt * 128:kbase + (kt + 1) * 128], ptb[:, h * 128:(h + 1) * 128],
                   [pb], [ktb[vbase + kt]])

    pass
    STOP = int(os.environ.get("KSTOP", "99"))

    class _Stop(Exception):
        pass

    def stage(k):
        if STOP <= k:
            raise _Stop()
    for l in range(L):
        for _ in range(len(blocks)):
            wplan.extend((l, si, parts[0][3]) for si, parts in enumerate(block_wplan(l)))
    try:
      stage(1)
      convert_weights(0)
      stage(2)
      for l in range(L):
          if l + 1 < L:
              convert_weights(l + 1)
          load_states(l)
          stage(3)
          for s0 in range(min(2, NS)):
              load_sample_cache(l, s0)
          for bi, blk in enumerate(blocks):
              run_block(l, blk)
              stage(10 + bi)
          end_of_layer_states(l)
    except _Stop:
        pass
    P.wait_all("sp", out_bufs)
    P.drain("sp")
    P.build()
    return nc, P


_CACHE = {}
_DBG_HOOK = None


def _get_program(cfg_key):
    if cfg_key not in _CACHE:
        cfg = Cfg(*cfg_key)
        _CACHE[cfg_key] = (cfg,) + build_program(cfg)
    return _CACHE[cfg_key]


def kernel(x_prompt, x_sample, cache_k, cache_v, state_conv, state_hgrn, state_ffn,
           meta_tokens, rel_bias, g_mix, w_in, g_q, g_k, lam_q1, lam_k1, lam_q2, lam_k2, g_diff,
           conv_w, conv_b, ln_g, ln_b, lb_logits, g_hgrn, w_out, g_ffn, w_up, ffn_conv_w, ffn_conv_b,
           w_down, g_final, _nb=256):
    f = lambda a: np.ascontiguousarray(np.asarray(a, dtype=np.float32))
    x_prompt, x_sample, cache_k, cache_v = f(x_prompt), f(x_sample), f(cache_k), f(cache_v)
    state_conv, state_hgrn, state_ffn = f(state_conv), f(state_hgrn), f(state_ffn)
    L = w_in.shape[0]
    BP, SEQ = x_prompt.shape[0], x_prompt.shape[1]
    BS = x_sample.shape[0]
    PAST = cache_k.shape[2]
    NCORES = 8
    NS = BS // NCORES
    cfg, nc, P = _get_program((L, SEQ, PAST, NS, _nb))
    sidx, nrows, ngrp = svec_layout(L)
    sv = np.zeros((ngrp * 128, 128), np.float32)

    def put(key, arr):
        a = f(arr).reshape(-1, 128)
        sv[sidx[key]:sidx[key] + a.shape[0]] = a
    for l in range(L):
        put(("g_mix", l), g_mix[l]); put(("g_ffn", l), g_ffn[l]); put(("g_diff", l), g_diff[l])
        put(("conv_b", l), conv_b[l]); put(("ln_g", l), ln_g[l]); put(("ln_b", l), ln_b[l])
        put(("g_hgrn", l), g_hgrn[l]); put(("lbl", l), lb_logits[l]); put(("fcb", l), ffn_conv_b[l])
        for t in range(3):
            put(("fcw", l, t), ffn_conv_w[l, t])
        for t in range(CW):
            put(("cw", l, t), conv_w[l, t])
        put(("g_q", l), np.concatenate([f(g_q[l]), f(g_q[l])]))
        put(("g_k", l), np.concatenate([f(g_k[l]), f(g_k[l])]))
    put(("g_final",), g_final)
    lamv = np.stack([f(lam_q1).reshape(-1), f(lam_k1).reshape(-1), f(lam_q2).reshape(-1), f(lam_k2).reshape(-1)])
    consts = host_consts()
    shared = {"meta": f(meta_tokens), "w_in": f(w_in), "w_out": f(w_out), "w_up": f(w_up), "w_down": f(w_down),
              "svec": sv, "relb": f(rel_bias), "lamv": lamv}
    for k, v in consts.items():
        shared["c_" + k] = v
    in_maps = []
    for c in range(NCORES):
        m = dict(shared)
        m["xp"] = x_prompt[c % BP]
        sl = slice(c * NS, (c + 1) * NS)
        m["xs"] = x_sample[sl].reshape(NS * 16, D)
        m["ck"] = np.ascontiguousarray(cache_k[:, sl].reshape(L, NS, PAST, 512))
        m["cv"] = np.ascontiguousarray(cache_v[:, sl].reshape(L, NS, PAST, 512))
        m["sc"] = np.ascontiguousarray(state_conv[:, sl])
        m["sh"] = np.ascontiguousarray(state_hgrn[:, sl].reshape(L, NS, 256, 64))
        m["sf"] = np.ascontiguousarray(state_ffn[:, sl])
        in_maps.append(m)
    res = run_bass_kernel_spmd(nc, in_maps, core_ids=list(range(NCORES)))
    R = res.results
    TP = NMETA + SEQ
    y_prompt = np.stack([R[b]["y_p"] for b in range(BP)])
    y_sample = np.concatenate([R[c]["y_s"].reshape(NS, 16, D) for c in range(NCORES)])
    k_prompt = np.stack([R[b]["k_p"] for b in range(BP)], axis=1).reshape(L, BP, TP, 4, 2, 64)
    v_prompt = np.stack([R[b]["v_p"] for b in range(BP)], axis=1).reshape(L, BP, TP, 4, 128)
    conv_prompt = np.stack([R[b]["conv_p"] for b in range(BP)], axis=1)
    hgrn_prompt = np.stack([R[b]["hgrn_p"] for b in range(BP)], axis=1).reshape(L, BP, 4, 64, 64)
    ffn_prompt = np.stack([R[b]["ffn_p"] for b in range(BP)], axis=1)
    cat = lambda name: np.concatenate([R[c][name] for c in range(NCORES)], axis=1)
    k_sample = cat("k_s").reshape(L, BS, 16, 4, 2, 64)
    v_sample = cat("v_s").reshape(L, BS, 16, 4, 128)
    conv_sample = cat("conv_s")
    hgrn_sample = cat("hgrn_s").reshape(L, BS, 4, 64, 64)
    ffn_sample = cat("ffn_s")
    if _DBG_HOOK is not None:
        _DBG_HOOK(R)
    return (y_prompt, y_sample, k_prompt, v_prompt, conv_prompt, hgrn_prompt, ffn_prompt,
            k_sample, v_sample, conv_sample, hgrn_sample, ffn_sample)
```
